# Optimizing a Trainium2 kernel written in Bass

```python
import math
import jax, jax.numpy as jnp
from jax import lax
import numpy as np

D_MODEL = 1024
BATCH = 4
SEQ = 4096
DEPTH = 1
DEC_BATCH = 8
DEC_SEQ = 4096
PAST_LEN = 128

HEAD_DIM = 64
N_HEADS = 8
N_KV = 2
GROUP = N_HEADS // N_KV
ATT_WIDTH = N_HEADS * HEAD_DIM
KV_WIDTH = N_KV * HEAD_DIM
GM_HEADS = 8
GM_HD = 64
GM_WIDTH = GM_HEADS * GM_HD
MIX_WIDTH = ATT_WIDTH + GM_WIDTH
IN_WIDTH = ATT_WIDTH + 2 * KV_WIDTH + 2 * GM_WIDTH
BLK = 128
WINDOW = 128
CHUNK = 128
N_BUCKETS = 32
MAX_DIST = 128
D_FF = int(math.ceil(8 * D_MODEL / 3 / 256) * 256)
EPS = 1e-6
NEG = -1e30

kernel_name = "hymba_style_window_gqa_gmlp_encoder"


def rms_norm(x, g):
    xf = x.astype(jnp.float32)
    y = xf * lax.rsqrt(jnp.mean(xf * xf, axis=-1, keepdims=True) + EPS)
    return (y * g.astype(jnp.float32)).astype(x.dtype)


def t5_buckets(rel):
    half = N_BUCKETS // 2
    max_exact = half // 2
    ret = (rel > 0).astype(np.int32) * half
    n = np.abs(rel)
    large = max_exact + (np.log(np.maximum(n, 1).astype(np.float32) / max_exact)
                         / np.log(MAX_DIST / max_exact) * (half - max_exact)).astype(np.int32)
    large = np.minimum(large, half - 1)
    return (ret + np.where(n < max_exact, n, large)).astype(np.int32)


def windowed_gqa(q, k, v, q_gain, k_gain, sink, rel_table):
    B, S, _ = q.shape
    nb = S // BLK
    q = rms_norm(q.reshape(B, S, N_KV, GROUP, HEAD_DIM), q_gain)
    k = rms_norm(k.reshape(B, S, N_KV, HEAD_DIM), k_gain)
    v = v.reshape(B, S, N_KV, HEAD_DIM)
    pad = ((0, 0), (BLK, BLK), (0, 0), (0, 0))
    kp = jnp.pad(k, pad).reshape(B, nb + 2, BLK, N_KV, HEAD_DIM)
    vp = jnp.pad(v, pad).reshape(B, nb + 2, BLK, N_KV, HEAD_DIM)
    kband = jnp.concatenate([kp[:, :-2], kp[:, 1:-1], kp[:, 2:]], axis=2)
    vband = jnp.concatenate([vp[:, :-2], vp[:, 1:-1], vp[:, 2:]], axis=2)
    qb = q.reshape(B, nb, BLK, N_KV, GROUP, HEAD_DIM)
    scores = jnp.einsum("bnqkgd,bnskd->bnkgqs", qb, kband).astype(jnp.float32)
    scores = scores * (1.0 / math.sqrt(HEAD_DIM))
    rel = (np.arange(3 * BLK) - BLK)[None, :] - np.arange(BLK)[:, None]
    bias = rel_table.astype(jnp.float32)[t5_buckets(rel)]
    bias = jnp.transpose(bias, (2, 0, 1)).reshape(N_KV, GROUP, BLK, 3 * BLK)
    band = np.abs(rel) <= WINDOW
    key_pos = np.arange(nb)[:, None] * BLK - BLK + np.arange(3 * BLK)[None, :]
    valid = (key_pos >= 0) & (key_pos < S)
    mask = band[None] & valid[:, None, :]
    scores = jnp.where(mask[None, :, None, None], scores + bias[None, None], NEG)
    s_sink = sink.astype(jnp.float32).reshape(N_KV, GROUP)[None, None, :, :, None, None]
    m = jnp.maximum(jnp.max(scores, axis=-1, keepdims=True), s_sink)
    e = jnp.exp(scores - m)
    probs = e / (jnp.sum(e, axis=-1, keepdims=True) + jnp.exp(s_sink - m))
    out = jnp.einsum("bnkgqs,bnskd->bnqkgd", probs.astype(v.dtype), vband)
    return out.reshape(B, S, ATT_WIDTH)


def chunked_spatial_gating(u, v, v_gain, w_s, b_s):
    B, S, _ = u.shape
    nc = S // CHUNK
    u = jax.nn.gelu(u)
    v = rms_norm(jax.nn.gelu(v), v_gain)
    vh = v.reshape(B, nc, CHUNK, GM_HEADS, GM_HD)
    s = jnp.einsum("hpq,bcqhd->bcphd", w_s, vh) + jnp.transpose(b_s)[None, None, :, :, None]
    return (u.reshape(B, nc, CHUNK, GM_HEADS, GM_HD) * s).reshape(B, S, GM_WIDTH)


def encoder_layer(x, rel_table, norm1, w_in, q_gain, k_gain, sink, v_gain, w_s, b_s,
                  attn_out_gain, gmlp_out_gain, w_o, norm2, w_gate, w_up, w_down):
    h = rms_norm(x, norm1)
    proj = h @ w_in
    o1 = ATT_WIDTH
    o2 = o1 + KV_WIDTH
    o3 = o2 + KV_WIDTH
    o4 = o3 + GM_WIDTH
    q, k, v = proj[..., :o1], proj[..., o1:o2], proj[..., o2:o3]
    gu, gv = proj[..., o3:o4], proj[..., o4:]
    a = rms_norm(windowed_gqa(q, k, v, q_gain, k_gain, sink, rel_table), attn_out_gain)
    g = rms_norm(chunked_spatial_gating(gu, gv, v_gain, w_s, b_s), gmlp_out_gain)
    x = x + jnp.concatenate([a, g], axis=-1) @ w_o
    h2 = rms_norm(x, norm2)
    x = x + (jax.nn.silu(h2 @ w_gate) * (h2 @ w_up)) @ w_down
    return x


def setup_inputs(seed: int = 0) -> dict:
    key = jax.random.key(seed)
    ks = jax.random.split(key, 20)
    f = jnp.float32
    nrm = lambda k, shape, s: jax.random.normal(k, shape, f) * s
    L = DEPTH
    return {
        "x_prompt": nrm(ks[0], (BATCH, SEQ, D_MODEL), 1.0),
        "x_sample": nrm(ks[1], (DEC_BATCH, DEC_SEQ, D_MODEL), 1.0),
        "rel_bias_table": nrm(ks[2], (N_BUCKETS, N_HEADS), 0.5),
        "norm1": 1.0 + nrm(ks[3], (L, D_MODEL), 0.05),
        "w_in": nrm(ks[4], (L, D_MODEL, IN_WIDTH), D_MODEL ** -0.5),
        "q_gain": 1.0 + nrm(ks[5], (L, HEAD_DIM), 0.05),
        "k_gain": 1.0 + nrm(ks[6], (L, HEAD_DIM), 0.05),
        "sink": nrm(ks[7], (L, N_HEADS), 0.5),
        "v_gain": 1.0 + nrm(ks[8], (L, GM_WIDTH), 0.05),
        "w_s": nrm(ks[9], (L, GM_HEADS, CHUNK, CHUNK), CHUNK ** -0.5),
        "b_s": 1.0 + nrm(ks[10], (L, GM_HEADS, CHUNK), 0.1),
        "attn_out_gain": 1.0 + nrm(ks[11], (L, ATT_WIDTH), 0.05),
        "gmlp_out_gain": 1.0 + nrm(ks[12], (L, GM_WIDTH), 0.05),
        "w_o": nrm(ks[13], (L, MIX_WIDTH, D_MODEL), MIX_WIDTH ** -0.5),
        "norm2": 1.0 + nrm(ks[14], (L, D_MODEL), 0.05),
        "w_gate": nrm(ks[15], (L, D_MODEL, D_FF), D_MODEL ** -0.5),
        "w_up": nrm(ks[16], (L, D_MODEL, D_FF), D_MODEL ** -0.5),
        "w_down": nrm(ks[17], (L, D_FF, D_MODEL), D_FF ** -0.5),
    }


def reference(x_prompt, x_sample, rel_bias_table, norm1, w_in, q_gain, k_gain, sink, v_gain,
              w_s, b_s, attn_out_gain, gmlp_out_gain, w_o, norm2, w_gate, w_up, w_down):
    y_prompt = x_prompt
    y_sample = x_sample
    for l in range(DEPTH):
        params = (rel_bias_table, norm1[l], w_in[l], q_gain[l], k_gain[l], sink[l], v_gain[l],
                  w_s[l], b_s[l], attn_out_gain[l], gmlp_out_gain[l], w_o[l], norm2[l],
                  w_gate[l], w_up[l], w_down[l])
        y_prompt = encoder_layer(y_prompt, *params)
        y_sample = encoder_layer(y_sample, *params)
    return (y_prompt, y_sample)
```

```python
import math
from contextlib import ExitStack

import numpy as np
import concourse.bass as bass
import concourse.mybir as mybir
from concourse.alu_op_type import AluOpType as ALU
from concourse.bass_utils import run_bass_kernel_spmd

F32 = mybir.dt.float32
BF16 = mybir.dt.bfloat16
AF = mybir.ActivationFunctionType
AX = mybir.AxisListType

D = 1024
LAYOUT = ["virt"] + ["main"] * 32 + ["virt", "halo"] + ["main"] * 16 + ["halo"]
NG = len(LAYOUT)
NM = 48
NX = 50
NF = 2
GINFO = []
_x = _m = _f = 0
for _k in LAYOUT:
    if _k == "virt":
        GINFO.append((None, _k, False, None, None))
    elif _k == "halo":
        GINFO.append((_x, _k, False, None, _f))
        _x += 1
        _f += 1
    else:
        GINFO.append((_x, _k, True, _m, None))
        _x += 1
        _m += 1
DFF = 2816
NFC = DFF // 128
EPS = 1e-6
GRP = 4
DEBUG_X1 = False
DEBUG_STEPS = None
DEBUG_STOP = None


class _Stop(Exception):
    pass


PSUM_KEYS = {"tr", "b1", "b2", "b3", "b4", "b5", "b6", "b7"}


STALL_LOG = None
XLAT = 0.35
XLAT_PE = 1.5


class Deferred:
    def __init__(self, meth, eng, name, args, kwargs):
        self.meth, self.eng, self.name, self.args, self.kwargs = meth, eng, name, args, kwargs

    def emit(self):
        return self.meth(*self.args, **self.kwargs)


class EngProxy:
    def __init__(self, real, eng):
        self._real, self._eng = real, eng

    def __getattr__(self, m):
        real_m = getattr(self._real, m)
        eng = self._eng

        def f(*a, **k):
            return Deferred(real_m, eng, m, a, k)
        return f


class NCProxy:
    def __init__(self, nc):
        self._nc = nc
        self.tensor = EngProxy(nc.tensor, "pe")
        self.vector = EngProxy(nc.vector, "dve")
        self.scalar = EngProxy(nc.scalar, "act")
        self.gpsimd = EngProxy(nc.gpsimd, "pool")
        self.sync = EngProxy(nc.sync, "sp")

    def __getattr__(self, a):
        return getattr(self._nc, a)


def _fsize(ap):
    n = 1
    for d in ap.shape[1:]:
        n *= d
    return n


def est_cost(d):
    k = d.kwargs
    if d.eng == "pe":
        if d.name == "transpose":
            return 0.08
        return max(0.036, _fsize(k["rhs"]) / 2400.0 + 0.012)
    if d.eng == "act":
        n = _fsize(k["out"])
        return 0.18 + n / 1250.0 + (0.09 if k.get("accum_out") is not None else 0.0)
    if d.eng == "dve":
        out = k["out"] if "out" in k else d.args[0]
        n = _fsize(out)
        c = 0.15 + n / 960.0
        if d.name in ("tensor_copy", "tensor_scalar"):
            c = 0.15 + n / 1920.0
        if d.name == "reciprocal":
            c = 0.2
        if d.name == "tensor_reduce":
            c = 0.15 + _fsize(k["in_"]) / 960.0
        if k.get("accum_out") is not None:
            c += 0.02
        return c
    if d.eng == "pool":
        out = k["out"] if "out" in k else d.args[0]
        n = _fsize(out)
        if d.name == "tensor_tensor":
            return 0.1 + n * 0.002
        return 0.25 + n * 0.0006
    return 0.06


def act_set(d):
    if d.eng != "act":
        return None
    f = d.kwargs.get("func")
    if f in (AF.Exp, AF.Ln):
        return "exp"
    if f == AF.Gelu_apprx_tanh:
        return "gelu"
    if f == AF.Silu:
        return "silu"
    return None


class Tracker:
    def __init__(self, nc, es):
        self.nc = nc
        self.es = es
        self.eng = {"pe": nc.tensor, "act": nc.scalar, "dve": nc.vector, "pool": nc.gpsimd, "sp": nc.sync}
        self.handle = {}
        self.count = {}
        for k in ("pe", "act", "dve", "pool"):
            self.handle["E:" + k] = es.enter_context(nc.semaphore("sem_" + k))
            self.count["E:" + k] = 0
        self.clock = {k: {} for k in self.eng}
        self.snap = {}
        self.last_w = {}
        self.readers = {}
        self.last_acc = {}
        self.rec = []
        self.sim_time = 0.0
        self.sim_log = []
        self.nwaits = 0
        self.nops = 0

    def _slot(self, slot):
        sid = "D:" + slot
        if sid not in self.handle:
            self.handle[sid] = self.es.enter_context(self.nc.semaphore("dsem_" + slot))
            self.count[sid] = 0
        return sid

    def op(self, eng, fn, reads=(), writes=(), slot=None):
        d = fn()
        self.rec.append((eng, d, tuple(reads), tuple(writes), slot))

    def flush(self):
        recs = self.rec
        self.rec = []
        n = len(recs)
        if n == 0:
            return
        deps = [None] * n
        lw, rd, la = {}, {}, {}
        for i, (eng, d, reads, writes, slot) in enumerate(recs):
            s = set()
            for k in reads + writes:
                if k in PSUM_KEYS:
                    if k in la:
                        s.add(la[k])
                    la[k] = i
            for k in reads:
                if k in PSUM_KEYS:
                    continue
                if k in lw:
                    s.add(lw[k])
            for k in writes:
                if k in PSUM_KEYS:
                    continue
                if k in lw:
                    s.add(lw[k])
                for j in rd.get(k, ()):
                    s.add(j)
            for k in writes:
                if k not in PSUM_KEYS:
                    lw[k] = i
                    rd[k] = []
            for k in reads:
                if k not in PSUM_KEYS:
                    rd.setdefault(k, []).append(i)
            s.discard(i)
            deps[i] = s
        users = [[] for _ in range(n)]
        ndep = [0] * n
        for i in range(n):
            ndep[i] = len(deps[i])
            for j in deps[i]:
                users[j].append(i)
        cost = [est_cost(r[1]) for r in recs]
        aset = [act_set(r[1]) for r in recs]
        engs = ("pe", "act", "dve", "pool", "sp")
        elig = {e: [] for e in engs}
        for i in range(n):
            if ndep[i] == 0:
                elig[recs[i][0]].append(i)
        free_at = {e: 0.0 for e in engs}
        cur_set = [None]
        dma_busy = [0.0]
        finish = [0.0] * n
        ready = [0.0] * n
        done = [False] * n
        order = []
        lo = 0
        WINDOW = 700
        nsched = 0
        while nsched < n:
            while lo < n and done[lo]:
                lo += 1
            best = None
            for e in engs:
                fa = free_at[e]
                for i in elig[e]:
                    if i > lo + WINDOW:
                        continue
                    st = ready[i] if ready[i] > fa else fa
                    if e == "act" and aset[i] is not None and aset[i] != cur_set[0]:
                        st += 1.3
                    key = (round(st / 0.25), i)
                    if best is None or key < best[0]:
                        best = (key, i, e, st)
            if best is None:
                for e in engs:
                    for i in elig[e]:
                        st = max(ready[i], free_at[e])
                        key = (i,)
                        if best is None or key < best[0]:
                            best = (key, i, e, st)
            _, i, e, st = best
            elig[e].remove(i)
            if STALL_LOG is not None and st > free_at[e] + 0.01 and deps[i]:
                j = max(deps[i], key=lambda q: finish[q])
                STALL_LOG.append((e, st - free_at[e], recs[j][0], recs[j][1].name, recs[j][3], recs[i][1].name,
                                  recs[i][2], recs[i][3], st))
            if e == "act" and aset[i] is not None:
                cur_set[0] = aset[i]
            if e == "sp":
                free_at[e] = st + 0.06
                out = recs[i][1].kwargs.get("out")
                esz = 2 if (out is not None and out.dtype == BF16) else 4
                nbytes = 128 * _fsize(out) * esz if out is not None else 0
                tx0 = max(st, dma_busy[0])
                dma_busy[0] = tx0 + nbytes / 1.9e5
                finish[i] = dma_busy[0] + 2.0
            else:
                free_at[e] = st + cost[i]
                finish[i] = st + cost[i]
            done[i] = True
            nsched += 1
            order.append((st, i))
            for u in users[i]:
                ndep[u] -= 1
                lat = (0.0 if e == "pe" else 0.08) if recs[u][0] == e else (XLAT_PE if recs[u][0] == "pe" else XLAT)
                if finish[i] + lat > ready[u]:
                    ready[u] = finish[i] + lat
                if ndep[u] == 0:
                    elig[recs[u][0]].append(u)
        order.sort()
        self.sim_time = max(finish)
        self.sim_log.append((n, self.sim_time))
        for st, i in order:
            eng, d, reads, writes, slot = recs[i]
            self._emit(eng, d, reads, writes, slot)

    def _emit(self, eng, dfr, reads=(), writes=(), slot=None):
        fn = dfr.emit
        need = {}

        def add(d):
            if d is not None and need.get(d[0], 0) < d[1]:
                need[d[0]] = d[1]

        excl = [k for k in list(reads) + list(writes) if k in PSUM_KEYS]
        reads = [k for k in reads if k not in PSUM_KEYS]
        writes = [k for k in writes if k not in PSUM_KEYS]
        for k in excl:
            d = self.last_acc.get(k)
            if d is not None and d[0] != "E:" + eng:
                add(d)
        for k in reads:
            add(self.last_w.get(k))
        for k in writes:
            add(self.last_w.get(k))
            for s, v in self.readers.get(k, {}).items():
                add((s, v))
        cl = self.clock[eng]
        waits = []
        for s, v in need.items():
            if eng == "pe" and s == "E:pe":
                continue
            if cl.get(s, 0) < v:
                waits.append((s, v))
        for s, v in waits:
            sn = self.snap.get((s, v))
            if sn is not None:
                for s2, v2 in sn.items():
                    if cl.get(s2, 0) < v2:
                        cl[s2] = v2
            if cl.get(s, 0) < v:
                cl[s] = v
        e = self.eng[eng]
        for s, v in waits[:-1]:
            e.wait_ge(self.handle[s], v)
        inst = fn()
        if waits:
            inst._wait_ge(self.handle[waits[-1][0]], waits[-1][1])
        self.nwaits += len(waits)
        self.nops += 1
        if slot is not None:
            sid = self._slot(slot)
            self.count[sid] += 1
            val = 16 * self.count[sid]
            inst.then_inc(self.handle[sid], 16)
        else:
            sid = "E:" + eng
            self.count[sid] += 1
            val = self.count[sid]
            inst.then_inc(self.handle[sid], 1)
        sn = dict(cl)
        sn[sid] = val
        self.snap[(sid, val)] = sn
        me = (sid, val)
        for k in excl:
            self.last_acc[k] = me
        for k in writes:
            self.last_w[k] = me
            self.readers[k] = {}
        for k in reads:
            r = self.readers.setdefault(k, {})
            if r.get(sid, 0) < val:
                r[sid] = val
        return inst

    def barrier(self):
        self.flush()
        for en, e in self.eng.items():
            cl = self.clock[en]
            for sid, c in self.count.items():
                v = c * 16 if sid.startswith("D:") else c
                if v > 0 and cl.get(sid, 0) < v:
                    e.wait_ge(self.handle[sid], v)
                    cl[sid] = v


def build_program():
    nc = bass.Bass("TRN2", target_bir_lowering=False)
    es = ExitStack()
    T = Tracker(nc, es)
    stacks = []
    try:
        _build_body(NCProxy(nc), es, T, stacks)
    except _Stop:
        pass
    T.barrier()
    for st in reversed(stacks):
        st.close()
    es.close()
    return nc, T


def _build_body(nc, es, T, stacks):
    def ck(name):
        if DEBUG_STOP == name:
            raise _Stop()


    def din(name, shape):
        return nc.dram_tensor(name, list(shape), F32, kind="ExternalInput").ap()

    xc = din("xc", [NX, 128, D])
    flags_d = din("flags", [128, NF])
    w_in_d = din("w_in", [D, 1792])
    w_o_d = din("w_o", [D, D])
    w_gate_d = din("w_gate", [D, DFF])
    w_up_d = din("w_up", [D, DFF])
    w_down_d = din("w_down", [DFF, D])
    g1_d = din("g1", [128, 8])
    g2_d = din("g2", [128, 8])
    gm_d = din("gm", [128, 8])
    qg_d = din("qg", [128, 1])
    kg_d = din("kg", [128, 1])
    vgb_d = din("vgb", [128, 512])
    bsb_d = din("bsb", [128, 512])
    wsT_d = din("wsT", [128, 8, 128])
    tab_d = din("tab", [32, 8])
    oht_d = din("oht", [33, 512])
    sinkb_d = din("sinkb", [128, 8])
    ident_d = din("ident", [128, 128])
    jrev_d = din("jrev", [128, 128])

    y_d = nc.dram_tensor("y", [NM * 128, D], F32, kind="ExternalOutput").ap()
    x1_h = nc.dram_tensor("x1d", [NM * 128, D], F32, kind="ExternalOutput" if DEBUG_X1 else "Internal")
    x1_d = x1_h.ap()
    txr_h = nc.dram_tensor("txr_d", [8, 512], F32)
    wg_s = nc.dram_tensor("wg_s", [128, 8, DFF], BF16).ap()
    wu_s = nc.dram_tensor("wu_s", [128, 8, DFF], BF16).ap()
    wd_s = nc.dram_tensor("wd_s", [128, NFC, D], BF16).ap()
    txr_d = txr_h.ap()

    def sb(name, shape, dt):
        return es.enter_context(nc.sbuf_tensor("s_" + name, list(shape), dt))

    identb = sb("identb", [128, 128], BF16)
    flags = sb("flags_sb", [128, NF], F32)
    flags2 = sb("flags2", [128, NF, 2], F32)
    ones2 = sb("ones2", [128, 2], F32)
    nhalf = sb("nhalf", [128, 16], F32)
    g2t = sb("g2t", [128, 8], F32)
    PFg = sb("PFg", [128, 8, 512], BF16)
    PFu = sb("PFu", [128, 8, 512], BF16)
    TRt = es.enter_context(nc.psum_tensor("tr", [128, 1024], BF16))
    PSt = es.enter_context(nc.psum_tensor("ps", [128, 7, 512], F32))
    TR = TRt
    PS = {f"b{i + 1}": PSt[:, i, :] for i in range(7)}
    ptmp = sb("ptmp", [128, 8, 16], F32)
    junkds = [sb(f"junkd{i}", [128, 1024], BF16) for i in range(3)]
    jctr = [0]
    junka = sb("junka", [128, 1024], BF16)

    def junk_next():
        i = jctr[0] % 3
        jctr[0] += 1
        return junkds[i], f"junkd{i}"
    pt_ctr = [0]

    def rsqrt(out, in_, mult, kin, kout, width):
        i = pt_ctr[0] % 8
        pt_ctr[0] += 1
        tmp = ptmp[:, i, 0:width]
        T.op("act", lambda: nc.scalar.activation(out=tmp, in_=in_, func=AF.Ln, scale=float(mult), bias=EPS),
             reads=[kin], writes=[f"ptmp{i}"])
        T.op("act", lambda: nc.scalar.activation(out=out, in_=tmp, func=AF.Exp, scale=-0.5),
             reads=[f"ptmp{i}"], writes=[kout])

    def load(dst, src, key, eng="sp"):
        q = nc.sync if eng == "sp" else nc.scalar
        T.op(eng, lambda: q.dma_start(out=dst, in_=src), writes=[key], slot=key)

    T.op("dve", lambda: nc.vector.memset(nhalf[:], -0.5), writes=["nhalf"])
    T.op("dve", lambda: nc.vector.memset(ones2[:], 1.0), writes=["ones2"])
    load(flags[:], flags_d[:, :], "flags")
    load(g2t[:], g2_d[:, :], "g2t")
    for i in range(2):
        T.op("dve", lambda: nc.vector.tensor_copy(out=flags2[:, :, i], in_=flags[:]), reads=["flags"],
             writes=["flags2"])

    ck('c0')
    esA = ExitStack()
    stacks.append(esA)

    def sbA(name, shape, dt):
        return esA.enter_context(nc.sbuf_tensor("a_" + name, list(shape), dt))

    Wi = sbA("Wi", [128, 8, 1792], BF16)
    Wo = sbA("Wo", [128, 8, 1024], BF16)
    wst = [sbA(f"wst{i}", [128, 1024], F32) for i in range(4)]
    wcb = [sbA(f"wcb{i}", [128, 1024], BF16) for i in range(2)]
    g1t = sbA("g1t", [128, 8], F32)
    gmt = sbA("gmt", [128, 8], F32)
    qgt = sbA("qgt", [128, 1], F32)
    kgt = sbA("kgt", [128, 1], F32)
    qkg = sbA("qkg", [128, 1], F32)
    vgb = sbA("vgb", [128, 512], F32)
    bsb = sbA("bsb", [128, 512], F32)
    wsf = sbA("wsf", [128, 8, 128], F32)
    wsb = sbA("wsb", [128, 8, 128], BF16)
    tab33 = sbA("tab33", [64, 8], F32)
    oht = sbA("oht", [64, 512], F32)
    txr = sbA("txr", [8, 512], F32)
    btf = sbA("btf", [128, 8, 128], F32)
    BT = sbA("BT", [128, 3, 8, 128], BF16)
    sinkb = sbA("sinkb", [128, 8], F32)
    esink = sbA("esink", [128, 8], F32)
    identf = sbA("identf", [128, 128], F32)
    jrevf = sbA("jrevf", [128, 128], F32)
    jrevb = sbA("jrevb", [128, 128], BF16)
    btb = sbA("btb", [128, 1024], BF16)

    xf = [sbA(f"xf{i}", [128, D], F32) for i in range(3)]
    hb = [sbA(f"h{i}", [128, D], BF16) for i in range(2)]
    hT = [sbA(f"hT{i}", [128, 8, 128], BF16) for i in range(2)]
    ss1 = sbA("ss1", [128, 4], F32)
    rs1 = sbA("rs1", [128, 4], F32)
    sq = [sbA(f"sq{i}", [128, 640], F32) for i in range(2)]
    ssq = [sbA(f"ssq{i}", [128, 10], F32) for i in range(2)]
    rqk = [sbA(f"rqk{i}", [128, 10], F32) for i in range(2)]
    qn = [sbA(f"qn{i}", [128, 512], BF16) for i in range(2)]
    kn = [sbA(f"kn{i}", [128, 128], BF16) for i in range(2)]
    QT = [sbA(f"QT{i}", [128, 4, 128], BF16) for i in range(3)]
    KT = sbA("KT", [128, 6, 128], BF16)
    VA = sbA("VA", [128, 6, 2, 65], BF16)
    gu = [sbA(f"gu{i}", [128, 512], F32) for i in range(2)]
    gv = [sbA(f"gv{i}", [128, 512], F32) for i in range(2)]
    vn = [sbA(f"vn{i}", [128, 512], BF16) for i in range(2)]
    graw = [sbA(f"graw{i}", [128, 512], F32) for i in range(2)]
    gmix = [sbA(f"gmix{i}", [128, 512], BF16) for i in range(6)]
    ssv = sbA("ssv", [128, 4], F32)
    rv = sbA("rv", [128, 4], F32)
    ssg = sbA("ssg", [128, 4], F32)
    rg = sbA("rg", [128, 4], F32)
    ET = [sbA(f"ET{i}", [128, 6, 512], BF16) for i in range(2)]
    den = [sbA(f"den{i}", [128, 8], F32) for i in range(2)]
    rden = [sbA(f"rden{i}", [128, 8], F32) for i in range(2)]
    araw = [sbA(f"araw{i}", [128, 512], F32) for i in range(2)]
    ssa = sbA("ssa", [128, 4], F32)
    ra = sbA("ra", [128, 4], F32)
    amix = [sbA(f"amix{i}", [128, 512], BF16) for i in range(2)]
    mixT = [sbA(f"mixT{i}", [128, 8, 128], BF16) for i in range(2)]
    xr = [sbA(f"xr{i}", [128, D], F32) for i in range(2)]

    load(g1t[:], g1_d[:, :], "g1t")
    cast_i = [0]

    def cast_fold(dst, src, gcol, rkeys, wkeys):
        i = cast_i[0] % 3
        cast_i[0] += 1
        if i == 0:
            T.op("dve", lambda: nc.vector.tensor_scalar(out=dst, in0=src, scalar1=gcol, scalar2=None,
                                                        op0=ALU.mult), reads=rkeys, writes=wkeys)
        elif i == 1:
            T.op("act", lambda: nc.scalar.activation(out=dst, in_=src, func=AF.Copy, scale=gcol),
                 reads=rkeys, writes=wkeys)
        else:
            T.op("pool", lambda: nc.gpsimd.tensor_scalar(out=dst, in0=src, scalar1=gcol, scalar2=1.0,
                                                         op0=ALU.mult, op1=ALU.mult), reads=rkeys, writes=wkeys)

    si = 0
    for kc in range(8):
        for (c0, c1) in ((0, 1024), (1024, 1792)):
            s_ = wst[si % 4]
            key = f"wst{si % 4}"
            si += 1
            load(s_[:, 0:c1 - c0], w_in_d[kc * 128:(kc + 1) * 128, c0:c1], key)
            cast_fold(Wi[:, kc, c0:c1], s_[:, 0:c1 - c0], g1t[:, kc:kc + 1], [key, "g1t"], [f"Wi{kc}"])
    load(identf[:], ident_d[:, :], "identf")
    T.op("dve", lambda: nc.vector.tensor_copy(out=identb[:], in_=identf[:]), reads=["identf"], writes=["identb"])
    load(gmt[:], gm_d[:, :], "gmt")
    load(qgt[:], qg_d[:, :], "qgt")
    load(kgt[:], kg_d[:, :], "kgt")
    T.op("dve", lambda: nc.vector.tensor_tensor(out=qkg[:], in0=qgt[:], in1=kgt[:], op=ALU.mult),
         reads=["qgt", "kgt"], writes=["qkg"])
    load(vgb[:], vgb_d[:, :], "vgb")
    load(bsb[:], bsb_d[:, :], "bsb")
    load(wsf[:], wsT_d[:, :, :], "wsf")
    T.op("dve", lambda: nc.vector.tensor_copy(out=wsb[:], in_=wsf[:]), reads=["wsf"], writes=["wsb"])
    load(sinkb[:], sinkb_d[:, :], "sinkb")
    T.op("act", lambda: nc.scalar.activation(out=esink[:], in_=sinkb[:], func=AF.Exp), reads=["sinkb"],
         writes=["esink"])

    ck('c1')
    T.op("dve", lambda: nc.vector.memset(tab33[:], -30000.0), writes=["tab33"])
    load(tab33[0:32, :], tab_d[:, :], "tab33")
    T.op("dve", lambda: nc.vector.memset(oht[:], 0.0), writes=["oht"])
    load(oht[0:33, :], oht_d[:, :], "oht")
    T.op("pe", lambda: nc.tensor.matmul(out=PS["b1"][0:8, 0:512], lhsT=tab33[0:64, :], rhs=oht[0:64, :],
                                        start=True, stop=True),
         reads=["tab33", "oht"], writes=["b1"])
    T.op("act", lambda: nc.scalar.activation(out=txr[:], in_=PS["b1"][0:8, 0:512], func=AF.Exp), reads=["b1"],
         writes=["txr"])
    T.op("sp", lambda: nc.sync.dma_start(out=txr_d[:, :], in_=txr[:]), reads=["txr"], writes=["txr_d"],
         slot="txr_st")
    load(jrevf[:], jrev_d[:, :], "jrevf")
    T.op("dve", lambda: nc.vector.tensor_copy(out=jrevb[:], in_=jrevf[:]), reads=["jrevf"], writes=["jrevb"])
    for j in range(3):
        src = bass.AP(txr_h, 128 - 128 * (j - 1), [[1, 128], [512, 8], [1, 128]])
        T.op("sp", lambda: nc.sync.dma_start(out=btf[:], in_=src), reads=["txr_d"], writes=["btf"], slot="btf")
        T.op("dve", lambda: nc.vector.tensor_copy(out=btb[:], in_=btf[:].rearrange("p h q -> p (h q)")),
             reads=["btf"], writes=["btb"])
        for hh in range(2):
            bank = "b2" if hh == 0 else "b3"
            T.op("pe", lambda: nc.tensor.matmul(out=PS[bank][:, :], lhsT=jrevb[:], rhs=btb[:, hh * 512:(hh + 1) * 512],
                                                start=True, stop=True), reads=["jrevb", "btb"], writes=[bank])
            T.op("dve", lambda: nc.vector.tensor_copy(
                out=BT[:, j, 4 * hh:4 * hh + 4, :].rearrange("p h q -> p (h q)"), in_=PS[bank][:, :]),
                 reads=[bank], writes=["BT"])

    ck('c2')
    for kc in range(8):
        s_ = wst[si % 4]
        key = f"wst{si % 4}"
        si += 1
        load(s_[:, 0:1024], w_o_d[kc * 128:(kc + 1) * 128, :], key)
        cast_fold(Wo[:, kc, :], s_[:, 0:1024], gmt[:, kc:kc + 1], [key, "gmt"], [f"Wo{kc}"])
    ck('c3')

    def ginfo(G):
        xi, kind, main, m, fidx = GINFO[G]
        return xi, fidx, main, m

    def real(G):
        return 0 <= G < NG and LAYOUT[G] != "virt"

    def Virt(G):
        slot = G % 6
        T.op("pool", lambda: nc.gpsimd.memset(VA[:, slot, :, :], 0.0), writes=[f"VA{slot}", f"VA{slot}o"])
        T.op("pool", lambda: nc.gpsimd.memset(KT[:, slot, :], 0.0), writes=[f"KT{slot}"])

    def A_load(G):
        s, kb, main, m = ginfo(G)
        r = G % 3
        T.op("sp", lambda: nc.sync.dma_start(out=xf[r][:], in_=xc[s]), writes=[f"xf{r}"], slot=f"xf{r}")

    def A_comp(G):
        r = G % 3
        c = G % 4
        T.op("act", lambda: nc.scalar.activation(out=junka[:], in_=xf[r][:], func=AF.Square,
                                                 accum_out=ss1[:, c:c + 1]),
             reads=[f"xf{r}"], writes=["junka", f"ss1_{c}"])
        rsqrt(rs1[:, c:c + 1], ss1[:, c:c + 1], 1.0 / D, f"ss1_{c}", f"rs1_{c}", 1)
        T.op("dve", lambda: nc.vector.tensor_scalar(out=hb[G % 2][:], in0=xf[r][:], scalar1=rs1[:, c:c + 1],
                                                    scalar2=None, op0=ALU.mult),
             reads=[f"xf{r}", f"rs1_{c}"], writes=[f"h{G % 2}"])

    def B(G):
        p = G % 2
        for i in range(8):
            T.op("pe", lambda: nc.tensor.transpose(out=TR[:, i * 128:(i + 1) * 128],
                                                   in_=hb[p][:, i * 128:(i + 1) * 128], identity=identb[:]),
                 reads=[f"h{p}", "identb"], writes=["tr"])
        ck('p3')
        T.op("dve", lambda: nc.vector.tensor_copy(out=hT[p][:, 0:4, :], in_=TR[:, 0:512]), reads=["tr"],
             writes=[f"hT{p}a"])
        ck('p4')
        T.op("dve", lambda: nc.vector.tensor_copy(out=hT[p][:, 4:8, :], in_=TR[:, 512:1024]), reads=["tr"],
             writes=[f"hT{p}b"])

    def C(G):
        s, kb, main, m = ginfo(G)
        p = G % 2
        c4 = G % 4
        slot = G % 6
        groups = [("b1", 0, 512), ("b2", 512, 768), ("b3", 768, 1280), ("b4", 1280, 1792)] if main else \
            [("b2", 512, 768)]
        for bank, c0, c1 in groups:
            for kc in range(8):
                T.op("pe", lambda: nc.tensor.matmul(out=PS[bank][:, 0:c1 - c0], lhsT=hT[p][:, kc, :],
                                                    rhs=Wi[:, kc, c0:c1], start=(kc == 0), stop=(kc == 7)),
                     reads=[f"hT{p}a", f"hT{p}b", f"Wi{kc}"], writes=[bank])
        if main:
            T.op("act", lambda: nc.scalar.activation(out=sq[p][:, 0:512], in_=PS["b1"][:, 0:512], func=AF.Square),
                 reads=["b1"], writes=[f"sq{p}a"])
        T.op("act", lambda: nc.scalar.activation(out=sq[p][:, 512:640], in_=PS["b2"][:, 0:128], func=AF.Square),
             reads=["b2"], writes=[f"sq{p}b"])
        h0 = 0 if main else 8
        nh = 10 - h0
        T.op("dve", lambda: nc.vector.tensor_reduce(
            out=ssq[p][:, h0:10], in_=sq[p][:, h0 * 64:640].rearrange("p (h d) -> p h d", d=64),
            axis=AX.X, op=ALU.add), reads=[f"sq{p}a", f"sq{p}b"], writes=[f"ssq{p}"])
        rsqrt(rqk[p][:, h0:10], ssq[p][:, h0:10], 1.0 / 64, f"ssq{p}", f"rqk{p}", nh)
        if main:
            T.op("dve", lambda: nc.vector.tensor_tensor(
                out=qn[p][:].rearrange("p (h d) -> p h d", d=64),
                in0=PS["b1"][:, 0:512].rearrange("p (h d) -> p h d", d=64),
                in1=rqk[p][:, 0:8].unsqueeze(2).to_broadcast([128, 8, 64]), op=ALU.mult),
                 reads=["b1", f"rqk{p}"], writes=[f"qn{p}"])
        T.op("dve", lambda: nc.vector.tensor_tensor(
            out=kn[p][:].rearrange("p (h d) -> p h d", d=64),
            in0=PS["b2"][:, 0:128].rearrange("p (h d) -> p h d", d=64),
            in1=rqk[p][:, 8:10].unsqueeze(2).to_broadcast([128, 2, 64]), op=ALU.mult),
             reads=["b2", f"rqk{p}"], writes=[f"kn{p}"])
        vsrc = PS["b2"][:, 128:256].rearrange("p (k d) -> p k d", d=64)
        if main:
            T.op("act", lambda: nc.scalar.activation(out=VA[:, slot, :, 0:64], in_=vsrc, func=AF.Copy), reads=["b2"],
                 writes=[f"VA{slot}"])
            T.op("pool", lambda: nc.gpsimd.tensor_copy(out=VA[:, slot, :, 64], in_=ones2[:]), reads=["ones2"],
                 writes=[f"VA{slot}o"])
        else:
            fi = kb
            T.op("act", lambda: nc.scalar.activation(out=VA[:, slot, :, 0:64], in_=vsrc, func=AF.Copy,
                                                     scale=flags[:, fi:fi + 1]),
                 reads=["b2", "flags"], writes=[f"VA{slot}"])
            T.op("pool", lambda: nc.gpsimd.tensor_copy(out=VA[:, slot, :, 64], in_=flags2[:, fi, :]),
                 reads=["flags2"], writes=[f"VA{slot}o"])
        if main:
            T.op("act", lambda: nc.scalar.activation(out=gu[p][:], in_=PS["b3"][:, :], func=AF.Gelu_apprx_tanh),
                 reads=["b3"], writes=[f"gu{p}"])
            T.op("act", lambda: nc.scalar.activation(out=gv[p][:], in_=PS["b4"][:, :], func=AF.Gelu_apprx_tanh),
                 reads=["b4"], writes=[f"gv{p}"])
            jk, jkey = junk_next()
            T.op("dve", lambda: nc.vector.scalar_tensor_tensor(out=jk[:, 0:512], in0=gv[p][:], scalar=1.0,
                                                               in1=gv[p][:], op0=ALU.mult, op1=ALU.mult,
                                                               accum_out=ssv[:, c4:c4 + 1]),
                 reads=[f"gv{p}"], writes=[jkey, f"ssv{c4}"])
            rsqrt(rv[:, c4:c4 + 1], ssv[:, c4:c4 + 1], 1.0 / 512, f"ssv{c4}", f"rv{c4}", 1)
            T.op("dve", lambda: nc.vector.scalar_tensor_tensor(out=vn[p][:], in0=gv[p][:], scalar=rv[:, c4:c4 + 1],
                                                               in1=vgb[:], op0=ALU.mult, op1=ALU.mult),
                 reads=[f"gv{p}", f"rv{c4}", "vgb"], writes=[f"vn{p}"])

    def Dst(G):
        s, kb, main, m = ginfo(G)
        p = G % 2
        c4 = G % 4
        slot = G % 6
        if main:
            for i in range(4):
                T.op("pe", lambda: nc.tensor.transpose(out=TR[:, i * 128:(i + 1) * 128],
                                                       in_=qn[p][:, i * 128:(i + 1) * 128], identity=identb[:]),
                     reads=[f"qn{p}", "identb"], writes=["tr"])
        T.op("pe", lambda: nc.tensor.transpose(out=TR[:, 512:640], in_=kn[p][:], identity=identb[:]),
             reads=[f"kn{p}", "identb"], writes=["tr"])
        if main:
            T.op("dve", lambda: nc.vector.tensor_copy(out=QT[G % 3][:].rearrange("p c t -> p (c t)"),
                                                      in_=TR[:, 0:512]), reads=["tr"], writes=[f"QT{G % 3}"])
        T.op("dve", lambda: nc.vector.tensor_scalar(out=KT[:, slot, :], in0=TR[:, 512:640], scalar1=qkg[:, 0:1],
                                                    scalar2=None, op0=ALU.mult),
             reads=["tr", "qkg"], writes=[f"KT{slot}"])

    def D2(G):
        s, kb, main, m = ginfo(G)
        p = G % 2
        c4 = G % 4
        if main:
            for h in range(8):
                T.op("pe", lambda: nc.tensor.matmul(out=PS["b5"][:, h * 64:(h + 1) * 64], lhsT=wsb[:, h, :],
                                                    rhs=vn[p][:, h * 64:(h + 1) * 64], start=True, stop=True),
                     reads=["wsb", f"vn{p}"], writes=["b5"])
            T.op("dve", lambda: nc.vector.tensor_tensor(out=graw[p][:], in0=PS["b5"][:, :], in1=bsb[:], op=ALU.add),
                 reads=["b5", "bsb"], writes=[f"graw{p}"])
            T.op("pool", lambda: nc.gpsimd.tensor_tensor(out=graw[p][:], in0=graw[p][:], in1=gu[p][:], op=ALU.mult),
                 reads=[f"graw{p}", f"gu{p}"], writes=[f"graw{p}"])
            jk, jkey = junk_next()
            T.op("dve", lambda: nc.vector.scalar_tensor_tensor(out=jk[:, 0:512], in0=graw[p][:], scalar=1.0,
                                                               in1=graw[p][:], op0=ALU.mult, op1=ALU.mult,
                                                               accum_out=ssg[:, c4:c4 + 1]),
                 reads=[f"graw{p}"], writes=[jkey, f"ssg{c4}"])
            rsqrt(rg[:, c4:c4 + 1], ssg[:, c4:c4 + 1], 1.0 / 512, f"ssg{c4}", f"rg{c4}", 1)
            gm = G % 6
            T.op("pool", lambda: nc.gpsimd.tensor_scalar(out=gmix[gm][:], in0=graw[p][:], scalar1=rg[:, c4:c4 + 1],
                                                         scalar2=1.0, op0=ALU.mult, op1=ALU.mult),
                 reads=[f"graw{p}", f"rg{c4}"], writes=[f"gmix{gm}"])

    def E1(c):
        e = c % 2
        for j in (-1, 0, 1):
            for kv in range(2):
                i = kv * 3 + j + 1
                bank = ("b5", "b6", "b7")[(2 * (j + 1) + kv) % 3]
                sl = (c + j) % 6
                T.op("pe", lambda: nc.tensor.matmul(out=PS[bank][:, :], lhsT=KT[64 * kv:64 * kv + 64, sl, :],
                                                    rhs=QT[c % 3][64 * kv:64 * kv + 64, :, :], start=True, stop=True),
                     reads=[f"KT{sl}", f"QT{c % 3}"], writes=[bank])
                T.op("act", lambda: nc.scalar.activation(out=ET[e][:, i, :], in_=PS[bank][:, :], func=AF.Exp,
                                                         scale=0.125),
                     reads=[bank], writes=[f"ET{e}_{i}"])
                T.op("pool", lambda: nc.gpsimd.tensor_tensor(
                    out=ET[e][:, i, :], in0=ET[e][:, i, :],
                    in1=BT[:, j + 1, 4 * kv:4 * kv + 4, :].rearrange("p h q -> p (h q)"), op=ALU.mult),
                     reads=[f"ET{e}_{i}", "BT"], writes=[f"ET{e}_{i}"])

    def E2(c):
        e = c % 2
        c4 = c % 4
        for kv in range(2):
            bank = "b6" if kv == 0 else "b7"
            for g in range(4):
                for j in (-1, 0, 1):
                    i = kv * 3 + j + 1
                    sl = (c + j) % 6
                    T.op("pe", lambda: nc.tensor.matmul(out=PS[bank][:, g * 65:(g + 1) * 65],
                                                        lhsT=ET[e][:, i, g * 128:(g + 1) * 128],
                                                        rhs=VA[:, sl, kv, :], start=(j == -1), stop=(j == 1)),
                         reads=[f"ET{e}_{i}", f"VA{sl}", f"VA{sl}o"], writes=[bank])
        for kv in range(2):
            bank = "b6" if kv == 0 else "b7"
            pv = PS[bank][:, 0:260].rearrange("p (g e) -> p g e", e=65)
            T.op("dve", lambda: nc.vector.tensor_tensor(out=den[e][:, 4 * kv:4 * kv + 4], in0=pv[:, :, 64],
                                                        in1=esink[:, 4 * kv:4 * kv + 4], op=ALU.add),
                 reads=[bank, "esink"], writes=[f"den{e}_{kv}"])
        T.op("dve", lambda: nc.vector.reciprocal(out=rden[e][:], in_=den[e][:]),
             reads=[f"den{e}_0", f"den{e}_1"], writes=[f"rden{e}"])
        for kv in range(2):
            bank = "b6" if kv == 0 else "b7"
            pv = PS[bank][:, 0:260].rearrange("p (g e) -> p g e", e=65)
            T.op("dve", lambda: nc.vector.tensor_tensor(
                out=araw[e][:, 256 * kv:256 * kv + 256].rearrange("p (g d) -> p g d", d=64), in0=pv[:, :, 0:64],
                in1=rden[e][:, 4 * kv:4 * kv + 4].unsqueeze(2).to_broadcast([128, 4, 64]), op=ALU.mult),
                 reads=[bank, f"rden{e}"], writes=[f"araw{e}_{kv}"])
        jk, jkey = junk_next()
        T.op("dve", lambda: nc.vector.scalar_tensor_tensor(out=jk[:, 0:512], in0=araw[e][:], scalar=1.0,
                                                           in1=araw[e][:], op0=ALU.mult, op1=ALU.mult,
                                                           accum_out=ssa[:, c4:c4 + 1]),
             reads=[f"araw{e}_0", f"araw{e}_1"], writes=[jkey, f"ssa{c4}"])
        rsqrt(ra[:, c4:c4 + 1], ssa[:, c4:c4 + 1], 1.0 / 512, f"ssa{c4}", f"ra{c4}", 1)
        T.op("pool", lambda: nc.gpsimd.tensor_scalar(out=amix[e][:], in0=araw[e][:], scalar1=ra[:, c4:c4 + 1],
                                                     scalar2=1.0, op0=ALU.mult, op1=ALU.mult),
             reads=[f"araw{e}_0", f"araw{e}_1", f"ra{c4}"], writes=[f"amix{e}"])

    def F1(c):
        e = c % 2
        gm = c % 6
        for i in range(8):
            src = amix[e][:, i * 128:(i + 1) * 128] if i < 4 else gmix[gm][:, (i - 4) * 128:(i - 3) * 128]
            T.op("pe", lambda: nc.tensor.transpose(out=TR[:, i * 128:(i + 1) * 128], in_=src, identity=identb[:]),
                 reads=[f"amix{e}", f"gmix{gm}", "identb"], writes=["tr"])
        T.op("dve", lambda: nc.vector.tensor_copy(out=mixT[e][:, 0:4, :], in_=TR[:, 0:512]), reads=["tr"],
             writes=[f"mixT{e}a"])
        T.op("dve", lambda: nc.vector.tensor_copy(out=mixT[e][:, 4:8, :], in_=TR[:, 512:1024]), reads=["tr"],
             writes=[f"mixT{e}b"])

    def XR_load(c):
        s, kb, main, m = ginfo(c)
        T.op("sp", lambda: nc.sync.dma_start(out=xr[m % 2][:], in_=xc[s]), writes=[f"xr{m % 2}"],
             slot=f"xr{m % 2}")

    def F2(c):
        s, kb, main, m = ginfo(c)
        e = c % 2
        r = m % 2
        for nh in range(2):
            bank = "b3" if nh == 0 else "b4"
            for kc in range(8):
                T.op("pe", lambda: nc.tensor.matmul(out=PS[bank][:, :], lhsT=mixT[e][:, kc, :],
                                                    rhs=Wo[:, kc, nh * 512:(nh + 1) * 512], start=(kc == 0),
                                                    stop=(kc == 7)),
                     reads=[f"mixT{e}a", f"mixT{e}b", f"Wo{kc}"], writes=[bank])
        for nh in range(2):
            bank = "b3" if nh == 0 else "b4"
            T.op("dve", lambda: nc.vector.tensor_tensor(out=xr[r][:, nh * 512:(nh + 1) * 512], in0=PS[bank][:, :],
                                                        in1=xr[r][:, nh * 512:(nh + 1) * 512], op=ALU.add),
                 reads=[bank, f"xr{r}"], writes=[f"xr{r}"])
        T.op("sp", lambda: nc.sync.dma_start(out=x1_d[m * 128:(m + 1) * 128, :], in_=xr[r][:]),
             reads=[f"xr{r}"], writes=[f"x1d{m}"], slot=f"xs{r}")

    wunits = []
    for (c0, c1) in ((0, 1024), (1024, 2048), (2048, DFF)):
        for wname in ("g", "u"):
            for kc in range(8):
                wunits.append((wname, kc, c0, c1))
    PIECES = ((0, 512), (512, 1024), (1024, 2048), (2048, DFF))
    for fc in range(NFC):
        wunits.append(("d", fc, 0, D))

    def W_load(u):
        wname, kc, c0, c1 = wunits[u]
        i = u % 4
        srcw = {"g": w_gate_d, "u": w_up_d, "d": w_down_d}[wname]
        n = c1 - c0
        T.op("sp", lambda: nc.sync.dma_start(out=wst[i][:, 0:n], in_=srcw[kc * 128:(kc + 1) * 128, c0:c1]),
             writes=[f"wst{i}"], slot=f"wst{i}")

    def W_cast(u):
        wname, kc, c0, c1 = wunits[u]
        i = u % 4
        j = u % 2
        n = c1 - c0
        sc = 1.0 if wname == "d" else g2t[:, kc:kc + 1]
        if u % 2 == 0:
            T.op("pool", lambda: nc.gpsimd.tensor_scalar(out=wcb[j][:, 0:n], in0=wst[i][:, 0:n], scalar1=sc,
                                                         scalar2=1.0, op0=ALU.mult, op1=ALU.mult),
                 reads=[f"wst{i}", "g2t"], writes=[f"wcb{j}"])
        else:
            T.op("act", lambda: nc.scalar.activation(out=wcb[j][:, 0:n], in_=wst[i][:, 0:n], func=AF.Copy, scale=sc),
                 reads=[f"wst{i}", "g2t"], writes=[f"wcb{j}"])

    def W_store(u):
        wname, kc, c0, c1 = wunits[u]
        j = u % 2
        dst = {"g": wg_s, "u": wu_s, "d": wd_s}[wname]
        n = c1 - c0
        T.op("sp", lambda: nc.sync.dma_start(out=dst[:, kc, c0:c1], in_=wcb[j][:, 0:n]),
             reads=[f"wcb{j}"], writes=[f"ws_{wname}_{kc}_{c0}"], slot=f"wcb{j}")

    def W_prefetch(kc):
        for wname, ws, pf in (("g", wg_s, PFg), ("u", wu_s, PFu)):
            T.op("sp", lambda: nc.sync.dma_start(out=pf[:, kc, :], in_=ws[:, kc, 0:512]),
                 reads=[f"ws_{wname}_{kc}_0"], writes=[f"W{wname}0"], slot=f"pf{wname}")

    def is_main(G):
        return 0 <= G < NG and LAYOUT[G] == "main"

    WLAG = 5
    ck('c4')
    for G in range(0, 3):
        if real(G):
            A_load(G)
    for G in range(0, 2):
        if real(G):
            A_comp(G)
    if real(0):
        B(0)
    for t in range(NG + 5 if DEBUG_STEPS is None else DEBUG_STEPS):
        if real(t + 3):
            A_load(t + 3)
        if is_main(t - 3):
            XR_load(t - 3)
        for uu in (2 * (t - WLAG), 2 * (t - WLAG) + 1):
            if 0 <= uu < len(wunits):
                W_load(uu)
        if is_main(t - 4):
            F1(t - 4)
        if real(t + 1):
            B(t + 1)
        if is_main(t - 2):
            E1(t - 2)
        if real(t):
            C(t)
        elif t < NG:
            Virt(t)
        if real(t + 2):
            A_comp(t + 2)
        if is_main(t - 1):
            D2(t - 1)
        if is_main(t - 3):
            E2(t - 3)
        if is_main(t - 4):
            F2(t - 4)
        if real(t):
            Dst(t)
        for uu in (2 * (t - WLAG - 2), 2 * (t - WLAG - 2) + 1):
            if 0 <= uu < len(wunits):
                W_store(uu)
        for uu in (2 * (t - WLAG - 1), 2 * (t - WLAG - 1) + 1):
            if 0 <= uu < len(wunits):
                W_cast(uu)
        if 0 <= t - (14 + WLAG) < 8:
            W_prefetch(t - (14 + WLAG))

    T.barrier()
    esA.close()
    stacks.pop()

    if not DEBUG_X1:
        Wgp = [PFg] + [sb(f"Wg{pi}", [128, 8, c1 - c0], BF16) for pi, (c0, c1) in enumerate(PIECES) if pi > 0]
        Wup = [PFu] + [sb(f"Wu{pi}", [128, 8, c1 - c0], BF16) for pi, (c0, c1) in enumerate(PIECES) if pi > 0]
        Wd = sb("Wd", [128, NFC, D], BF16)
        x1p = [sb(f"x1p{i}", [128, D], F32) for i in range(7)]
        h2 = [sb(f"h2_{i}", [128, D], BF16) for i in range(2)]
        h2T = sb("h2T", [128, 8, GRP * 128], BF16)
        actT = sb("actT", [128, NFC, GRP * 128], BF16)
        sg = [sb(f"sg{i}", [128, GRP * 128], F32) for i in range(2)]
        ss2 = sb("ss2", [128, 4], F32)
        rs2 = sb("rs2", [128, 4], F32)

        NGRP = NM // GRP
        banksB = ["b1", "b2", "b3", "b4", "b5", "b6", "b7"]
        bctr = [0]

        def nbank():
            b = banksB[bctr[0] % 7]
            bctr[0] += 1
            return b

        def P_load(m):
            r = m % 7
            T.op("sp", lambda: nc.sync.dma_start(out=x1p[r][:], in_=x1_d[m * 128:(m + 1) * 128, :]),
                 reads=[f"x1d{m}"], writes=[f"x1p{r}"], slot=f"x1p{r}")

        def P_norm(m):
            r = m % 7
            c = m % 4
            jk, jkey = junk_next()
            T.op("dve", lambda: nc.vector.scalar_tensor_tensor(out=jk[:], in0=x1p[r][:], scalar=1.0,
                                                               in1=x1p[r][:], op0=ALU.mult, op1=ALU.mult,
                                                               accum_out=ss2[:, c:c + 1]),
                 reads=[f"x1p{r}"], writes=[jkey, f"ss2_{c}"])
            rsqrt(rs2[:, c:c + 1], ss2[:, c:c + 1], 1.0 / D, f"ss2_{c}", f"rs2_{c}", 1)
            T.op("dve", lambda: nc.vector.tensor_scalar(out=h2[m % 2][:], in0=x1p[r][:], scalar1=rs2[:, c:c + 1],
                                                        scalar2=None, op0=ALU.mult),
                 reads=[f"x1p{r}", f"rs2_{c}"], writes=[f"h2_{m % 2}"])

        def P_tr(m):
            i4 = m % GRP
            for i in range(8):
                T.op("pe", lambda: nc.tensor.transpose(out=TR[:, i * 128:(i + 1) * 128],
                                                       in_=h2[m % 2][:, i * 128:(i + 1) * 128], identity=identb[:]),
                     reads=[f"h2_{m % 2}", "identb"], writes=["tr"])
            T.op("dve", lambda: nc.vector.tensor_copy(
                out=h2T[:, 0:4, i4 * 128:(i4 + 1) * 128],
                in_=TR[:, 0:512].rearrange("p (c t) -> p c t", t=128)), reads=["tr"], writes=["h2Ta"])
            T.op("dve", lambda: nc.vector.tensor_copy(
                out=h2T[:, 4:8, i4 * 128:(i4 + 1) * 128],
                in_=TR[:, 512:1024].rearrange("p (c t) -> p c t", t=128)), reads=["tr"], writes=["h2Tb"])

        def GU(g):
            for fc in range(NFC):
                pi = [p_ for p_, (c0, c1) in enumerate(PIECES) if c0 <= fc * 128 < c1][0]
                pc0 = PIECES[pi][0]
                bg = nbank()
                bu = nbank()
                for bank, wt, wname in ((bg, Wgp[pi], "Wg"), (bu, Wup[pi], "Wu")):
                    for kc in range(8):
                        T.op("pe", lambda: nc.tensor.matmul(out=PS[bank][:, :],
                                                            lhsT=wt[:, kc, fc * 128 - pc0:(fc + 1) * 128 - pc0],
                                                            rhs=h2T[:, kc, :], start=(kc == 0), stop=(kc == 7)),
                             reads=[f"{wname}{pi}", "h2Ta", "h2Tb"], writes=[bank])
                T.op("act", lambda: nc.scalar.activation(out=sg[fc % 2][:], in_=PS[bg][:, :], func=AF.Silu),
                     reads=[bg], writes=[f"sg{fc % 2}"])
                T.op("dve", lambda: nc.vector.tensor_tensor(out=actT[:, fc, :], in0=PS[bu][:, :], in1=sg[fc % 2][:],
                                                            op=ALU.mult),
                     reads=[bu, f"sg{fc % 2}"], writes=[f"actT{fc}"])

        def DN_tile(g, i4):
            m = g * GRP + i4
            r = m % 7
            bks = [nbank(), nbank()]
            for nh in range(2):
                for fc in range(NFC):
                    T.op("pe", lambda: nc.tensor.matmul(out=PS[bks[nh]][:, :],
                                                        lhsT=actT[:, fc, i4 * 128:(i4 + 1) * 128],
                                                        rhs=Wd[:, fc, nh * 512:(nh + 1) * 512], start=(fc == 0),
                                                        stop=(fc == NFC - 1)),
                         reads=[f"actT{fc}", f"Wd{fc // 11}"], writes=[bks[nh]])
            for nh in range(2):
                T.op("dve", lambda: nc.vector.tensor_tensor(out=x1p[r][:, nh * 512:(nh + 1) * 512],
                                                            in0=PS[bks[nh]][:, :],
                                                            in1=x1p[r][:, nh * 512:(nh + 1) * 512], op=ALU.add),
                     reads=[bks[nh], f"x1p{r}"], writes=[f"x1p{r}"])
            T.op("sp", lambda: nc.sync.dma_start(out=y_d[m * 128:(m + 1) * 128, :], in_=x1p[r][:]),
                 reads=[f"x1p{r}"], writes=[f"y{m}"], slot=f"ys{r}")

        for m in range(GRP):
            P_load(m)
        for pi, (c0, c1) in enumerate(PIECES):
            if pi == 0:
                continue
            for wname, ws, wt in (("g", wg_s, Wgp[pi]), ("u", wu_s, Wup[pi])):
                T.op("sp", lambda: nc.sync.dma_start(out=wt[:], in_=ws[:, :, c0:c1]),
                     writes=[f"W{wname}{pi}"], slot=f"W{wname}{pi}")
        for hh in range(2):
            T.op("sp", lambda: nc.sync.dma_start(out=Wd[:, hh * 11:(hh + 1) * 11, :], in_=wd_s[:, hh * 11:(hh + 1) * 11, :]),
                 writes=[f"Wd{hh}"], slot=f"Wd{hh}")

        for m in range(GRP):
            P_norm(m)
            P_tr(m)
        for g in range(NGRP):
            nxt = g + 1 < NGRP
            m0 = (g + 1) * GRP
            if nxt:
                for i4 in range(3):
                    P_load(m0 + i4)
            GU(g)
            if nxt:
                for i4 in range(3):
                    P_norm(m0 + i4)
                    P_tr(m0 + i4)
            for i4 in range(GRP):
                DN_tile(g, i4)
                if nxt and i4 == 0:
                    P_load(m0 + 3)
                    P_norm(m0 + 3)
                    P_tr(m0 + 3)

    return


def t5_buckets_np(rel):
    half = 16
    max_exact = 8
    ret = (rel > 0).astype(np.int32) * half
    n = np.abs(rel)
    large = max_exact + (np.log(np.maximum(n, 1).astype(np.float32) / max_exact)
                         / np.log(128 / max_exact) * (half - max_exact)).astype(np.int32)
    large = np.minimum(large, half - 1)
    return (ret + np.where(n < max_exact, n, large)).astype(np.int32)


_CACHE = {}


def kernel(x_prompt, x_sample, rel_bias_table, norm1, w_in, q_gain, k_gain, sink, v_gain, w_s, b_s,
           attn_out_gain, gmlp_out_gain, w_o, norm2, w_gate, w_up, w_down):
    f = np.float32
    x_prompt = np.asarray(x_prompt, f)
    x_sample = np.asarray(x_sample, f)
    seqs = [x_prompt[i] for i in range(x_prompt.shape[0])] + [x_sample[i] for i in range(x_sample.shape[0])]
    assert len(seqs) == 12

    w_in0 = np.asarray(w_in, f)[0]
    hperm = [0, 4, 1, 5, 2, 6, 3, 7]
    qcols = np.concatenate([np.arange(h * 64, (h + 1) * 64) for h in hperm])
    w_in_p = np.ascontiguousarray(np.concatenate([w_in0[:, qcols], w_in0[:, 512:]], axis=1))
    col = lambda v: np.ascontiguousarray(np.asarray(v, f).reshape(8, 128).T)
    g1 = col(np.asarray(norm1, f)[0])
    g2 = col(np.asarray(norm2, f)[0])
    gm = col(np.concatenate([np.asarray(attn_out_gain, f)[0], np.asarray(gmlp_out_gain, f)[0]]))
    qg = np.ascontiguousarray(np.tile(np.asarray(q_gain, f)[0], 2).reshape(128, 1))
    kg = np.ascontiguousarray(np.tile(np.asarray(k_gain, f)[0], 2).reshape(128, 1))
    vgb = np.ascontiguousarray(np.broadcast_to(np.asarray(v_gain, f)[0][None, :], (128, 512)))
    bsb = np.ascontiguousarray(np.repeat(np.asarray(b_s, f)[0].T[:, :, None], 64, axis=2).reshape(128, 512))
    wsT = np.ascontiguousarray(np.transpose(np.asarray(w_s, f)[0], (2, 0, 1)))
    tab = np.ascontiguousarray(np.asarray(rel_bias_table, f))
    sinkb = np.ascontiguousarray(np.broadcast_to(np.asarray(sink, f)[0][None, :], (128, 8)))
    ident = np.eye(128, dtype=f)
    oht = np.zeros((33, 512), f)
    for i in range(512):
        r = 255 - i
        if abs(r) <= 128:
            oht[int(t5_buckets_np(np.array([r]))[0]), i] = 1.0
        else:
            oht[32, i] = 1.0

    shared = dict(w_in=w_in_p, w_o=np.ascontiguousarray(np.asarray(w_o, f)[0]),
                  w_gate=np.ascontiguousarray(np.asarray(w_gate, f)[0]),
                  w_up=np.ascontiguousarray(np.asarray(w_up, f)[0]),
                  w_down=np.ascontiguousarray(np.asarray(w_down, f)[0]),
                  g1=g1, g2=g2, gm=gm, qg=qg, kg=kg, vgb=vgb, bsb=bsb, wsT=wsT, tab=tab, oht=oht,
                  sinkb=sinkb, ident=ident, jrev=np.ascontiguousarray(ident[::-1]))

    in_maps = []
    for c in range(8):
        xcore = np.zeros((NX, 128, D), f)
        flags = np.zeros((128, NF), f)
        xcore[0:32] = seqs[c].reshape(32, 128, D)
        hf = c % 2
        xs = seqs[8 + c // 2].reshape(32, 128, D)
        xcore[33:49] = xs[16 * hf:16 * hf + 16]
        if hf == 1:
            xcore[32] = xs[15]
            flags[:, 0] = 1.0
        else:
            xcore[49] = xs[16]
            flags[:, 1] = 1.0
        m = dict(shared)
        m["xc"] = xcore
        m["flags"] = flags
        in_maps.append(m)

    if "nc" not in _CACHE:
        _CACHE["nc"] = build_program()[0]
    nc = _CACHE["nc"]
    res = run_bass_kernel_spmd(nc, in_maps, core_ids=list(range(8)))
    key = "x1d" if DEBUG_X1 else "y"
    outs = [np.asarray(r[key]).reshape(NM * 128, D) for r in res.results]
    full = [np.zeros((4096, D), f) for _ in seqs]
    for c in range(8):
        hf = c % 2
        full[c][:] = outs[c][0:4096]
        full[8 + c // 2][2048 * hf:2048 * hf + 2048] = outs[c][4096:6144]
    yp = np.stack(full[:x_prompt.shape[0]]).astype(f)
    ys = np.stack(full[x_prompt.shape[0]:]).astype(f)
    return (yp, ys)
```

```python
import math
from contextlib import ExitStack

import numpy as np
import concourse.bass as bass
import concourse.mybir as mybir
from concourse.alu_op_type import AluOpType as ALU
from concourse.bass_utils import run_bass_kernel_spmd

F32 = mybir.dt.float32
BF16 = mybir.dt.bfloat16
AF = mybir.ActivationFunctionType
AX = mybir.AxisListType

D = 1024
LAYOUT = ["halo"] + ["main"] * 16 + ["halo", "virt"] + ["main"] * 32 + ["virt"]
NG = len(LAYOUT)
NM = 48
NX = 50
NF = 2
GINFO = []
_x = _m = _f = 0
for _k in LAYOUT:
    if _k == "virt":
        GINFO.append((None, _k, False, None, None))
    elif _k == "halo":
        GINFO.append((_x, _k, False, None, _f))
        _x += 1
        _f += 1
    else:
        GINFO.append((_x, _k, True, _m, None))
        _x += 1
        _m += 1
DFF = 2816
NFC = DFF // 128
EPS = 1e-6
GRP = 4
DEBUG_X1 = False
DEBUG_STEPS = None
DEBUG_STOP = None


class _Stop(Exception):
    pass


PSUM_KEYS = {"tr", "b1", "b2", "b3", "b4", "b5", "b6", "b7"}


STALL_LOG = None
XLAT = 0.35
XLAT_PE = 1.5


class Deferred:
    def __init__(self, meth, eng, name, args, kwargs):
        self.meth, self.eng, self.name, self.args, self.kwargs = meth, eng, name, args, kwargs

    def emit(self):
        return self.meth(*self.args, **self.kwargs)


class EngProxy:
    def __init__(self, real, eng):
        self._real, self._eng = real, eng

    def __getattr__(self, m):
        real_m = getattr(self._real, m)
        eng = self._eng

        def f(*a, **k):
            return Deferred(real_m, eng, m, a, k)
        return f


class NCProxy:
    def __init__(self, nc):
        self._nc = nc
        self.tensor = EngProxy(nc.tensor, "pe")
        self.vector = EngProxy(nc.vector, "dve")
        self.scalar = EngProxy(nc.scalar, "act")
        self.gpsimd = EngProxy(nc.gpsimd, "pool")
        self.sync = EngProxy(nc.sync, "sp")

    def __getattr__(self, a):
        return getattr(self._nc, a)


def _fsize(ap):
    n = 1
    for d in ap.shape[1:]:
        n *= d
    return n


def est_cost(d):
    k = d.kwargs
    if d.eng == "pe":
        if d.name == "transpose":
            return 0.08
        return max(0.036, _fsize(k["rhs"]) / 2400.0 + 0.012)
    if d.eng == "act":
        n = _fsize(k["out"])
        return 0.18 + n / 1250.0 + (0.09 if k.get("accum_out") is not None else 0.0)
    if d.eng == "dve":
        out = k["out"] if "out" in k else d.args[0]
        n = _fsize(out)
        c = 0.15 + n / 960.0
        if d.name in ("tensor_copy", "tensor_scalar"):
            c = 0.15 + n / 1920.0
        if d.name == "reciprocal":
            c = 0.2
        if d.name == "tensor_reduce":
            c = 0.15 + _fsize(k["in_"]) / 960.0
        if k.get("accum_out") is not None:
            c += 0.02
        return c
    if d.eng == "pool":
        out = k["out"] if "out" in k else d.args[0]
        n = _fsize(out)
        if d.name == "tensor_tensor":
            return 0.1 + n * 0.002
        return 0.25 + n * 0.0006
    return 0.06


def act_set(d):
    if d.eng != "act":
        return None
    f = d.kwargs.get("func")
    if f in (AF.Exp, AF.Ln):
        return "exp"
    if f == AF.Gelu_apprx_tanh:
        return "gelu"
    if f == AF.Silu:
        return "silu"
    return None


class Tracker:
    def __init__(self, nc, es):
        self.nc = nc
        self.es = es
        self.eng = {"pe": nc.tensor, "act": nc.scalar, "dve": nc.vector, "pool": nc.gpsimd, "sp": nc.sync}
        self.handle = {}
        self.count = {}
        for k in ("pe", "act", "dve", "pool"):
            self.handle["E:" + k] = es.enter_context(nc.semaphore("sem_" + k))
            self.count["E:" + k] = 0
        self.clock = {k: {} for k in self.eng}
        self.snap = {}
        self.last_w = {}
        self.readers = {}
        self.last_acc = {}
        self.rec = []
        self.sim_time = 0.0
        self.sim_log = []
        self.nwaits = 0
        self.nops = 0

    def _slot(self, slot):
        sid = "D:" + slot
        if sid not in self.handle:
            self.handle[sid] = self.es.enter_context(self.nc.semaphore("dsem_" + slot))
            self.count[sid] = 0
        return sid

    def op(self, eng, fn, reads=(), writes=(), slot=None):
        d = fn()
        self.rec.append((eng, d, tuple(reads), tuple(writes), slot))

    def flush(self):
        recs = self.rec
        self.rec = []
        n = len(recs)
        if n == 0:
            return
        deps = [None] * n
        lw, rd, la = {}, {}, {}
        for i, (eng, d, reads, writes, slot) in enumerate(recs):
            s = set()
            for k in reads + writes:
                if k in PSUM_KEYS:
                    if k in la:
                        s.add(la[k])
                    la[k] = i
            for k in reads:
                if k in PSUM_KEYS:
                    continue
                if k in lw:
                    s.add(lw[k])
            for k in writes:
                if k in PSUM_KEYS:
                    continue
                if k in lw:
                    s.add(lw[k])
                for j in rd.get(k, ()):
                    s.add(j)
            for k in writes:
                if k not in PSUM_KEYS:
                    lw[k] = i
                    rd[k] = []
            for k in reads:
                if k not in PSUM_KEYS:
                    rd.setdefault(k, []).append(i)
            s.discard(i)
            deps[i] = s
        users = [[] for _ in range(n)]
        ndep = [0] * n
        for i in range(n):
            ndep[i] = len(deps[i])
            for j in deps[i]:
                users[j].append(i)
        cost = [est_cost(r[1]) for r in recs]
        aset = [act_set(r[1]) for r in recs]
        engs = ("pe", "act", "dve", "pool", "sp")
        elig = {e: [] for e in engs}
        for i in range(n):
            if ndep[i] == 0:
                elig[recs[i][0]].append(i)
        free_at = {e: 0.0 for e in engs}
        cur_set = [None]
        dma_busy = [0.0]
        finish = [0.0] * n
        ready = [0.0] * n
        done = [False] * n
        order = []
        lo = 0
        WINDOW = 700
        nsched = 0
        while nsched < n:
            while lo < n and done[lo]:
                lo += 1
            best = None
            for e in engs:
                fa = free_at[e]
                for i in elig[e]:
                    if i > lo + WINDOW:
                        continue
                    st = ready[i] if ready[i] > fa else fa
                    if e == "act" and aset[i] is not None and aset[i] != cur_set[0]:
                        st += 1.3
                    key = (round(st / 0.25), i)
                    if best is None or key < best[0]:
                        best = (key, i, e, st)
            if best is None:
                for e in engs:
                    for i in elig[e]:
                        st = max(ready[i], free_at[e])
                        key = (i,)
                        if best is None or key < best[0]:
                            best = (key, i, e, st)
            _, i, e, st = best
            elig[e].remove(i)
            if STALL_LOG is not None and st > free_at[e] + 0.01 and deps[i]:
                j = max(deps[i], key=lambda q: finish[q])
                STALL_LOG.append((e, st - free_at[e], recs[j][0], recs[j][1].name, recs[j][3], recs[i][1].name,
                                  recs[i][2], recs[i][3], st))
            if e == "act" and aset[i] is not None:
                cur_set[0] = aset[i]
            if e == "sp":
                free_at[e] = st + 0.06
                out = recs[i][1].kwargs.get("out")
                esz = 2 if (out is not None and out.dtype == BF16) else 4
                nbytes = 128 * _fsize(out) * esz if out is not None else 0
                tx0 = max(st, dma_busy[0])
                dma_busy[0] = tx0 + nbytes / 1.9e5
                finish[i] = dma_busy[0] + 2.0
            else:
                free_at[e] = st + cost[i]
                finish[i] = st + cost[i]
            done[i] = True
            nsched += 1
            order.append((st, i))
            for u in users[i]:
                ndep[u] -= 1
                lat = (0.0 if e == "pe" else 0.08) if recs[u][0] == e else (XLAT_PE if recs[u][0] == "pe" else XLAT)
                if finish[i] + lat > ready[u]:
                    ready[u] = finish[i] + lat
                if ndep[u] == 0:
                    elig[recs[u][0]].append(u)
        order.sort()
        self.sim_time = max(finish)
        self.sim_log.append((n, self.sim_time))
        for st, i in order:
            eng, d, reads, writes, slot = recs[i]
            self._emit(eng, d, reads, writes, slot)

    def _emit(self, eng, dfr, reads=(), writes=(), slot=None):
        fn = dfr.emit
        need = {}

        def add(d):
            if d is not None and need.get(d[0], 0) < d[1]:
                need[d[0]] = d[1]

        excl = [k for k in list(reads) + list(writes) if k in PSUM_KEYS]
        reads = [k for k in reads if k not in PSUM_KEYS]
        writes = [k for k in writes if k not in PSUM_KEYS]
        for k in excl:
            d = self.last_acc.get(k)
            if d is not None and d[0] != "E:" + eng:
                add(d)
        for k in reads:
            add(self.last_w.get(k))
        for k in writes:
            add(self.last_w.get(k))
            for s, v in self.readers.get(k, {}).items():
                add((s, v))
        cl = self.clock[eng]
        waits = []
        for s, v in need.items():
            if eng == "pe" and s == "E:pe":
                continue
            if cl.get(s, 0) < v:
                waits.append((s, v))
        for s, v in waits:
            sn = self.snap.get((s, v))
            if sn is not None:
                for s2, v2 in sn.items():
                    if cl.get(s2, 0) < v2:
                        cl[s2] = v2
            if cl.get(s, 0) < v:
                cl[s] = v
        e = self.eng[eng]
        for s, v in waits[:-1]:
            e.wait_ge(self.handle[s], v)
        inst = fn()
        if waits:
            inst._wait_ge(self.handle[waits[-1][0]], waits[-1][1])
        self.nwaits += len(waits)
        self.nops += 1
        if slot is not None:
            sid = self._slot(slot)
            self.count[sid] += 1
            val = 16 * self.count[sid]
            inst.then_inc(self.handle[sid], 16)
        else:
            sid = "E:" + eng
            self.count[sid] += 1
            val = self.count[sid]
            inst.then_inc(self.handle[sid], 1)
        sn = dict(cl)
        sn[sid] = val
        self.snap[(sid, val)] = sn
        me = (sid, val)
        for k in excl:
            self.last_acc[k] = me
        for k in writes:
            self.last_w[k] = me
            self.readers[k] = {}
        for k in reads:
            r = self.readers.setdefault(k, {})
            if r.get(sid, 0) < val:
                r[sid] = val
        return inst

    def barrier(self):
        self.flush()
        for en, e in self.eng.items():
            cl = self.clock[en]
            for sid, c in self.count.items():
                v = c * 16 if sid.startswith("D:") else c
                if v > 0 and cl.get(sid, 0) < v:
                    e.wait_ge(self.handle[sid], v)
                    cl[sid] = v


def build_program():
    nc = bass.Bass("TRN2", target_bir_lowering=False)
    es = ExitStack()
    T = Tracker(nc, es)
    stacks = []
    try:
        _build_body(NCProxy(nc), es, T, stacks)
    except _Stop:
        pass
    T.barrier()
    for st in reversed(stacks):
        st.close()
    es.close()
    return nc, T


def _build_body(nc, es, T, stacks):
    def ck(name):
        if DEBUG_STOP == name:
            raise _Stop()


    def din(name, shape):
        return nc.dram_tensor(name, list(shape), F32, kind="ExternalInput").ap()

    xc = din("xc", [NX, 128, D])
    flags_d = din("flags", [128, NF])
    w_in_d = din("w_in", [D, 1792])
    w_o_d = din("w_o", [D, D])
    w_gate_d = din("w_gate", [D, DFF])
    w_up_d = din("w_up", [D, DFF])
    w_down_d = din("w_down", [DFF, D])
    g1_d = din("g1", [128, 8])
    g2_d = din("g2", [128, 8])
    gm_d = din("gm", [128, 8])
    qg_d = din("qg", [128, 1])
    kg_d = din("kg", [128, 1])
    vgb_d = din("vgb", [128, 512])
    bsb_d = din("bsb", [128, 512])
    wsT_d = din("wsT", [128, 8, 128])
    tab_d = din("tab", [32, 8])
    oht_d = din("oht", [33, 512])
    sinkb_d = din("sinkb", [128, 8])
    ident_d = din("ident", [128, 128])
    jrev_d = din("jrev", [128, 128])

    y_d = nc.dram_tensor("y", [NM * 128, D], F32, kind="ExternalOutput").ap()
    x1_h = nc.dram_tensor("x1d", [NM * 128, D], F32, kind="ExternalOutput" if DEBUG_X1 else "Internal")
    x1_d = x1_h.ap()
    txr_h = nc.dram_tensor("txr_d", [8, 512], F32)
    wg_s = nc.dram_tensor("wg_s", [128, 8, DFF], BF16).ap()
    wu_s = nc.dram_tensor("wu_s", [128, 8, DFF], BF16).ap()
    wd_s = nc.dram_tensor("wd_s", [128, NFC, D], BF16).ap()
    txr_d = txr_h.ap()

    def sb(name, shape, dt):
        return es.enter_context(nc.sbuf_tensor("s_" + name, list(shape), dt))

    identb = sb("identb", [128, 128], BF16)
    flags = sb("flags_sb", [128, NF], F32)
    flags2 = sb("flags2", [128, NF, 2], F32)
    ones2 = sb("ones2", [128, 2], F32)
    nhalf = sb("nhalf", [128, 16], F32)
    g2t = sb("g2t", [128, 8], F32)
    PFg = sb("PFg", [128, 8, 512], BF16)
    PFu = sb("PFu", [128, 8, 512], BF16)
    TRt = es.enter_context(nc.psum_tensor("tr", [128, 1024], BF16))
    PSt = es.enter_context(nc.psum_tensor("ps", [128, 7, 512], F32))
    TR = TRt
    PS = {f"b{i + 1}": PSt[:, i, :] for i in range(7)}
    ptmp = sb("ptmp", [128, 8, 16], F32)
    junkds = [sb(f"junkd{i}", [128, 1024], BF16) for i in range(3)]
    jctr = [0]
    junka = sb("junka", [128, 1024], BF16)

    def junk_next():
        i = jctr[0] % 3
        jctr[0] += 1
        return junkds[i], f"junkd{i}"
    pt_ctr = [0]

    def rsqrt(out, in_, mult, kin, kout, width):
        i = pt_ctr[0] % 8
        pt_ctr[0] += 1
        tmp = ptmp[:, i, 0:width]
        T.op("act", lambda: nc.scalar.activation(out=tmp, in_=in_, func=AF.Ln, scale=float(mult), bias=EPS),
             reads=[kin], writes=[f"ptmp{i}"])
        T.op("act", lambda: nc.scalar.activation(out=out, in_=tmp, func=AF.Exp, scale=-0.5),
             reads=[f"ptmp{i}"], writes=[kout])

    def load(dst, src, key, eng="sp"):
        q = nc.sync if eng == "sp" else nc.scalar
        T.op(eng, lambda: q.dma_start(out=dst, in_=src), writes=[key], slot=key)

    T.op("dve", lambda: nc.vector.memset(nhalf[:], -0.5), writes=["nhalf"])
    T.op("dve", lambda: nc.vector.memset(ones2[:], 1.0), writes=["ones2"])
    load(flags[:], flags_d[:, :], "flags")
    load(g2t[:], g2_d[:, :], "g2t")
    for i in range(2):
        T.op("dve", lambda: nc.vector.tensor_copy(out=flags2[:, :, i], in_=flags[:]), reads=["flags"],
             writes=["flags2"])

    ck('c0')
    esA = ExitStack()
    stacks.append(esA)

    def sbA(name, shape, dt):
        return esA.enter_context(nc.sbuf_tensor("a_" + name, list(shape), dt))

    Wi = sbA("Wi", [128, 8, 1792], BF16)
    Wo = sbA("Wo", [128, 8, 1024], BF16)
    wst = [sbA(f"wst{i}", [128, 1024], F32) for i in range(4)]
    wcb = [sbA(f"wcb{i}", [128, 1024], BF16) for i in range(2)]
    g1t = sbA("g1t", [128, 8], F32)
    gmt = sbA("gmt", [128, 8], F32)
    qgt = sbA("qgt", [128, 1], F32)
    kgt = sbA("kgt", [128, 1], F32)
    qkg = sbA("qkg", [128, 1], F32)
    vgb = sbA("vgb", [128, 512], F32)
    bsb = sbA("bsb", [128, 512], F32)
    wsf = sbA("wsf", [128, 8, 128], F32)
    wsb = sbA("wsb", [128, 8, 128], BF16)
    tab33 = sbA("tab33", [64, 8], F32)
    oht = sbA("oht", [64, 512], F32)
    txr = sbA("txr", [8, 512], F32)
    btf = sbA("btf", [128, 8, 128], F32)
    BT = sbA("BT", [128, 3, 8, 128], BF16)
    sinkb = sbA("sinkb", [128, 8], F32)
    esink = sbA("esink", [128, 8], F32)
    identf = sbA("identf", [128, 128], F32)
    jrevf = sbA("jrevf", [128, 128], F32)
    jrevb = sbA("jrevb", [128, 128], BF16)
    btb = sbA("btb", [128, 1024], BF16)

    xf = [sbA(f"xf{i}", [128, D], F32) for i in range(3)]
    hb = [sbA(f"h{i}", [128, D], BF16) for i in range(2)]
    hT = [sbA(f"hT{i}", [128, 8, 128], BF16) for i in range(2)]
    ss1 = sbA("ss1", [128, 4], F32)
    rs1 = sbA("rs1", [128, 4], F32)
    sq = [sbA(f"sq{i}", [128, 640], F32) for i in range(2)]
    ssq = [sbA(f"ssq{i}", [128, 10], F32) for i in range(2)]
    rqk = [sbA(f"rqk{i}", [128, 10], F32) for i in range(2)]
    qn = [sbA(f"qn{i}", [128, 512], BF16) for i in range(2)]
    kn = [sbA(f"kn{i}", [128, 128], BF16) for i in range(2)]
    QT = [sbA(f"QT{i}", [128, 4, 128], BF16) for i in range(3)]
    KT = sbA("KT", [128, 6, 128], BF16)
    VA = sbA("VA", [128, 6, 2, 65], BF16)
    gu = [sbA(f"gu{i}", [128, 512], F32) for i in range(2)]
    gv = [sbA(f"gv{i}", [128, 512], F32) for i in range(2)]
    vn = [sbA(f"vn{i}", [128, 512], BF16) for i in range(2)]
    graw = [sbA(f"graw{i}", [128, 512], F32) for i in range(2)]
    gmix = [sbA(f"gmix{i}", [128, 512], BF16) for i in range(6)]
    ssv = sbA("ssv", [128, 4], F32)
    rv = sbA("rv", [128, 4], F32)
    ssg = sbA("ssg", [128, 4], F32)
    rg = sbA("rg", [128, 4], F32)
    ET = [sbA(f"ET{i}", [128, 6, 512], BF16) for i in range(2)]
    den = [sbA(f"den{i}", [128, 8], F32) for i in range(2)]
    rden = [sbA(f"rden{i}", [128, 8], F32) for i in range(2)]
    araw = [sbA(f"araw{i}", [128, 512], F32) for i in range(2)]
    ssa = sbA("ssa", [128, 4], F32)
    ra = sbA("ra", [128, 4], F32)
    amix = [sbA(f"amix{i}", [128, 512], BF16) for i in range(2)]
    mixT = [sbA(f"mixT{i}", [128, 8, 128], BF16) for i in range(2)]
    xr = [sbA(f"xr{i}", [128, D], F32) for i in range(2)]

    load(g1t[:], g1_d[:, :], "g1t")
    cast_i = [0]

    def cast_fold(dst, src, gcol, rkeys, wkeys):
        i = cast_i[0] % 3
        cast_i[0] += 1
        if i == 0:
            T.op("dve", lambda: nc.vector.tensor_scalar(out=dst, in0=src, scalar1=gcol, scalar2=None,
                                                        op0=ALU.mult), reads=rkeys, writes=wkeys)
        elif i == 1:
            T.op("act", lambda: nc.scalar.activation(out=dst, in_=src, func=AF.Copy, scale=gcol),
                 reads=rkeys, writes=wkeys)
        else:
            T.op("pool", lambda: nc.gpsimd.tensor_scalar(out=dst, in0=src, scalar1=gcol, scalar2=1.0,
                                                         op0=ALU.mult, op1=ALU.mult), reads=rkeys, writes=wkeys)

    si = 0
    for kc in range(8):
        for (c0, c1) in ((0, 1024), (1024, 1792)):
            s_ = wst[si % 4]
            key = f"wst{si % 4}"
            si += 1
            load(s_[:, 0:c1 - c0], w_in_d[kc * 128:(kc + 1) * 128, c0:c1], key)
            cast_fold(Wi[:, kc, c0:c1], s_[:, 0:c1 - c0], g1t[:, kc:kc + 1], [key, "g1t"], [f"Wi{kc}"])
    load(identf[:], ident_d[:, :], "identf")
    T.op("dve", lambda: nc.vector.tensor_copy(out=identb[:], in_=identf[:]), reads=["identf"], writes=["identb"])
    load(gmt[:], gm_d[:, :], "gmt")
    load(qgt[:], qg_d[:, :], "qgt")
    load(kgt[:], kg_d[:, :], "kgt")
    T.op("dve", lambda: nc.vector.tensor_tensor(out=qkg[:], in0=qgt[:], in1=kgt[:], op=ALU.mult),
         reads=["qgt", "kgt"], writes=["qkg"])
    load(vgb[:], vgb_d[:, :], "vgb")
    load(bsb[:], bsb_d[:, :], "bsb")
    load(wsf[:], wsT_d[:, :, :], "wsf")
    T.op("dve", lambda: nc.vector.tensor_copy(out=wsb[:], in_=wsf[:]), reads=["wsf"], writes=["wsb"])
    load(sinkb[:], sinkb_d[:, :], "sinkb")
    T.op("act", lambda: nc.scalar.activation(out=esink[:], in_=sinkb[:], func=AF.Exp), reads=["sinkb"],
         writes=["esink"])

    ck('c1')
    T.op("dve", lambda: nc.vector.memset(tab33[:], -30000.0), writes=["tab33"])
    load(tab33[0:32, :], tab_d[:, :], "tab33")
    T.op("dve", lambda: nc.vector.memset(oht[:], 0.0), writes=["oht"])
    load(oht[0:33, :], oht_d[:, :], "oht")
    T.op("pe", lambda: nc.tensor.matmul(out=PS["b1"][0:8, 0:512], lhsT=tab33[0:64, :], rhs=oht[0:64, :],
                                        start=True, stop=True),
         reads=["tab33", "oht"], writes=["b1"])
    T.op("act", lambda: nc.scalar.activation(out=txr[:], in_=PS["b1"][0:8, 0:512], func=AF.Exp), reads=["b1"],
         writes=["txr"])
    T.op("sp", lambda: nc.sync.dma_start(out=txr_d[:, :], in_=txr[:]), reads=["txr"], writes=["txr_d"],
         slot="txr_st")
    load(jrevf[:], jrev_d[:, :], "jrevf")
    T.op("dve", lambda: nc.vector.tensor_copy(out=jrevb[:], in_=jrevf[:]), reads=["jrevf"], writes=["jrevb"])
    for j in range(3):
        src = bass.AP(txr_h, 128 - 128 * (j - 1), [[1, 128], [512, 8], [1, 128]])
        T.op("sp", lambda: nc.sync.dma_start(out=btf[:], in_=src), reads=["txr_d"], writes=["btf"], slot="btf")
        T.op("dve", lambda: nc.vector.tensor_copy(out=btb[:], in_=btf[:].rearrange("p h q -> p (h q)")),
             reads=["btf"], writes=["btb"])
        for hh in range(2):
            bank = "b2" if hh == 0 else "b3"
            T.op("pe", lambda: nc.tensor.matmul(out=PS[bank][:, :], lhsT=jrevb[:], rhs=btb[:, hh * 512:(hh + 1) * 512],
                                                start=True, stop=True), reads=["jrevb", "btb"], writes=[bank])
            T.op("dve", lambda: nc.vector.tensor_copy(
                out=BT[:, j, 4 * hh:4 * hh + 4, :].rearrange("p h q -> p (h q)"), in_=PS[bank][:, :]),
                 reads=[bank], writes=["BT"])

    ck('c2')
    for kc in range(8):
        s_ = wst[si % 4]
        key = f"wst{si % 4}"
        si += 1
        load(s_[:, 0:1024], w_o_d[kc * 128:(kc + 1) * 128, :], key)
        cast_fold(Wo[:, kc, :], s_[:, 0:1024], gmt[:, kc:kc + 1], [key, "gmt"], [f"Wo{kc}"])
    ck('c3')

    def ginfo(G):
        xi, kind, main, m, fidx = GINFO[G]
        return xi, fidx, main, m

    def real(G):
        return 0 <= G < NG and LAYOUT[G] != "virt"

    def Virt(G):
        slot = G % 6
        T.op("pool", lambda: nc.gpsimd.memset(VA[:, slot, :, :], 0.0), writes=[f"VA{slot}", f"VA{slot}o"])
        T.op("pool", lambda: nc.gpsimd.memset(KT[:, slot, :], 0.0), writes=[f"KT{slot}"])

    def A_load(G):
        s, kb, main, m = ginfo(G)
        r = G % 3
        T.op("sp", lambda: nc.sync.dma_start(out=xf[r][:], in_=xc[s]), writes=[f"xf{r}"], slot=f"xf{r}")

    def A_comp(G):
        r = G % 3
        c = G % 4
        T.op("act", lambda: nc.scalar.activation(out=junka[:], in_=xf[r][:], func=AF.Square,
                                                 accum_out=ss1[:, c:c + 1]),
             reads=[f"xf{r}"], writes=["junka", f"ss1_{c}"])
        rsqrt(rs1[:, c:c + 1], ss1[:, c:c + 1], 1.0 / D, f"ss1_{c}", f"rs1_{c}", 1)
        T.op("dve", lambda: nc.vector.tensor_scalar(out=hb[G % 2][:], in0=xf[r][:], scalar1=rs1[:, c:c + 1],
                                                    scalar2=None, op0=ALU.mult),
             reads=[f"xf{r}", f"rs1_{c}"], writes=[f"h{G % 2}"])

    def B(G):
        p = G % 2
        for i in range(8):
            T.op("pe", lambda: nc.tensor.transpose(out=TR[:, i * 128:(i + 1) * 128],
                                                   in_=hb[p][:, i * 128:(i + 1) * 128], identity=identb[:]),
                 reads=[f"h{p}", "identb"], writes=["tr"])
        ck('p3')
        T.op("dve", lambda: nc.vector.tensor_copy(out=hT[p][:, 0:4, :], in_=TR[:, 0:512]), reads=["tr"],
             writes=[f"hT{p}a"])
        ck('p4')
        T.op("dve", lambda: nc.vector.tensor_copy(out=hT[p][:, 4:8, :], in_=TR[:, 512:1024]), reads=["tr"],
             writes=[f"hT{p}b"])

    def C(G):
        s, kb, main, m = ginfo(G)
        p = G % 2
        c4 = G % 4
        slot = G % 6
        groups = [("b1", 0, 512), ("b2", 512, 768), ("b3", 768, 1280), ("b4", 1280, 1792)] if main else \
            [("b2", 512, 768)]
        for bank, c0, c1 in groups:
            for kc in range(8):
                T.op("pe", lambda: nc.tensor.matmul(out=PS[bank][:, 0:c1 - c0], lhsT=hT[p][:, kc, :],
                                                    rhs=Wi[:, kc, c0:c1], start=(kc == 0), stop=(kc == 7)),
                     reads=[f"hT{p}a", f"hT{p}b", f"Wi{kc}"], writes=[bank])
        if main:
            T.op("act", lambda: nc.scalar.activation(out=sq[p][:, 0:512], in_=PS["b1"][:, 0:512], func=AF.Square),
                 reads=["b1"], writes=[f"sq{p}a"])
        T.op("act", lambda: nc.scalar.activation(out=sq[p][:, 512:640], in_=PS["b2"][:, 0:128], func=AF.Square),
             reads=["b2"], writes=[f"sq{p}b"])
        h0 = 0 if main else 8
        nh = 10 - h0
        T.op("dve", lambda: nc.vector.tensor_reduce(
            out=ssq[p][:, h0:10], in_=sq[p][:, h0 * 64:640].rearrange("p (h d) -> p h d", d=64),
            axis=AX.X, op=ALU.add), reads=[f"sq{p}a", f"sq{p}b"], writes=[f"ssq{p}"])
        rsqrt(rqk[p][:, h0:10], ssq[p][:, h0:10], 1.0 / 64, f"ssq{p}", f"rqk{p}", nh)
        if main:
            T.op("dve", lambda: nc.vector.tensor_tensor(
                out=qn[p][:].rearrange("p (h d) -> p h d", d=64),
                in0=PS["b1"][:, 0:512].rearrange("p (h d) -> p h d", d=64),
                in1=rqk[p][:, 0:8].unsqueeze(2).to_broadcast([128, 8, 64]), op=ALU.mult),
                 reads=["b1", f"rqk{p}"], writes=[f"qn{p}"])
        T.op("dve", lambda: nc.vector.tensor_tensor(
            out=kn[p][:].rearrange("p (h d) -> p h d", d=64),
            in0=PS["b2"][:, 0:128].rearrange("p (h d) -> p h d", d=64),
            in1=rqk[p][:, 8:10].unsqueeze(2).to_broadcast([128, 2, 64]), op=ALU.mult),
             reads=["b2", f"rqk{p}"], writes=[f"kn{p}"])
        vsrc = PS["b2"][:, 128:256].rearrange("p (k d) -> p k d", d=64)
        if main:
            T.op("act", lambda: nc.scalar.activation(out=VA[:, slot, :, 0:64], in_=vsrc, func=AF.Copy), reads=["b2"],
                 writes=[f"VA{slot}"])
            T.op("pool", lambda: nc.gpsimd.tensor_copy(out=VA[:, slot, :, 64], in_=ones2[:]), reads=["ones2"],
                 writes=[f"VA{slot}o"])
        else:
            fi = kb
            T.op("act", lambda: nc.scalar.activation(out=VA[:, slot, :, 0:64], in_=vsrc, func=AF.Copy,
                                                     scale=flags[:, fi:fi + 1]),
                 reads=["b2", "flags"], writes=[f"VA{slot}"])
            T.op("pool", lambda: nc.gpsimd.tensor_copy(out=VA[:, slot, :, 64], in_=flags2[:, fi, :]),
                 reads=["flags2"], writes=[f"VA{slot}o"])
        if main:
            T.op("act", lambda: nc.scalar.activation(out=gu[p][:], in_=PS["b3"][:, :], func=AF.Gelu_apprx_tanh),
                 reads=["b3"], writes=[f"gu{p}"])
            T.op("act", lambda: nc.scalar.activation(out=gv[p][:], in_=PS["b4"][:, :], func=AF.Gelu_apprx_tanh),
                 reads=["b4"], writes=[f"gv{p}"])
            jk, jkey = junk_next()
            T.op("dve", lambda: nc.vector.scalar_tensor_tensor(out=jk[:, 0:512], in0=gv[p][:], scalar=1.0,
                                                               in1=gv[p][:], op0=ALU.mult, op1=ALU.mult,
                                                               accum_out=ssv[:, c4:c4 + 1]),
                 reads=[f"gv{p}"], writes=[jkey, f"ssv{c4}"])
            rsqrt(rv[:, c4:c4 + 1], ssv[:, c4:c4 + 1], 1.0 / 512, f"ssv{c4}", f"rv{c4}", 1)
            T.op("dve", lambda: nc.vector.scalar_tensor_tensor(out=vn[p][:], in0=gv[p][:], scalar=rv[:, c4:c4 + 1],
                                                               in1=vgb[:], op0=ALU.mult, op1=ALU.mult),
                 reads=[f"gv{p}", f"rv{c4}", "vgb"], writes=[f"vn{p}"])

    def Dst(G):
        s, kb, main, m = ginfo(G)
        p = G % 2
        c4 = G % 4
        slot = G % 6
        if main:
            for i in range(4):
                T.op("pe", lambda: nc.tensor.transpose(out=TR[:, i * 128:(i + 1) * 128],
                                                       in_=qn[p][:, i * 128:(i + 1) * 128], identity=identb[:]),
                     reads=[f"qn{p}", "identb"], writes=["tr"])
        T.op("pe", lambda: nc.tensor.transpose(out=TR[:, 512:640], in_=kn[p][:], identity=identb[:]),
             reads=[f"kn{p}", "identb"], writes=["tr"])
        if main:
            T.op("dve", lambda: nc.vector.tensor_copy(out=QT[G % 3][:].rearrange("p c t -> p (c t)"),
                                                      in_=TR[:, 0:512]), reads=["tr"], writes=[f"QT{G % 3}"])
        T.op("dve", lambda: nc.vector.tensor_scalar(out=KT[:, slot, :], in0=TR[:, 512:640], scalar1=qkg[:, 0:1],
                                                    scalar2=None, op0=ALU.mult),
             reads=["tr", "qkg"], writes=[f"KT{slot}"])

    def D2(G):
        s, kb, main, m = ginfo(G)
        p = G % 2
        c4 = G % 4
        if main:
            for h in range(8):
                T.op("pe", lambda: nc.tensor.matmul(out=PS["b5"][:, h * 64:(h + 1) * 64], lhsT=wsb[:, h, :],
                                                    rhs=vn[p][:, h * 64:(h + 1) * 64], start=True, stop=True),
                     reads=["wsb", f"vn{p}"], writes=["b5"])
            T.op("dve", lambda: nc.vector.tensor_tensor(out=graw[p][:], in0=PS["b5"][:, :], in1=bsb[:], op=ALU.add),
                 reads=["b5", "bsb"], writes=[f"graw{p}"])
            T.op("pool", lambda: nc.gpsimd.tensor_tensor(out=graw[p][:], in0=graw[p][:], in1=gu[p][:], op=ALU.mult),
                 reads=[f"graw{p}", f"gu{p}"], writes=[f"graw{p}"])
            jk, jkey = junk_next()
            T.op("dve", lambda: nc.vector.scalar_tensor_tensor(out=jk[:, 0:512], in0=graw[p][:], scalar=1.0,
                                                               in1=graw[p][:], op0=ALU.mult, op1=ALU.mult,
                                                               accum_out=ssg[:, c4:c4 + 1]),
                 reads=[f"graw{p}"], writes=[jkey, f"ssg{c4}"])
            rsqrt(rg[:, c4:c4 + 1], ssg[:, c4:c4 + 1], 1.0 / 512, f"ssg{c4}", f"rg{c4}", 1)
            gm = G % 6
            T.op("pool", lambda: nc.gpsimd.tensor_scalar(out=gmix[gm][:], in0=graw[p][:], scalar1=rg[:, c4:c4 + 1],
                                                         scalar2=1.0, op0=ALU.mult, op1=ALU.mult),
                 reads=[f"graw{p}", f"rg{c4}"], writes=[f"gmix{gm}"])

    def E1(c):
        e = c % 2
        for j in (-1, 0, 1):
            for kv in range(2):
                i = kv * 3 + j + 1
                bank = ("b5", "b6", "b7")[(2 * (j + 1) + kv) % 3]
                sl = (c + j) % 6
                T.op("pe", lambda: nc.tensor.matmul(out=PS[bank][:, :], lhsT=KT[64 * kv:64 * kv + 64, sl, :],
                                                    rhs=QT[c % 3][64 * kv:64 * kv + 64, :, :], start=True, stop=True),
                     reads=[f"KT{sl}", f"QT{c % 3}"], writes=[bank])
                T.op("act", lambda: nc.scalar.activation(out=ET[e][:, i, :], in_=PS[bank][:, :], func=AF.Exp,
                                                         scale=0.125),
                     reads=[bank], writes=[f"ET{e}_{i}"])
                T.op("pool", lambda: nc.gpsimd.tensor_tensor(
                    out=ET[e][:, i, :], in0=ET[e][:, i, :],
                    in1=BT[:, j + 1, 4 * kv:4 * kv + 4, :].rearrange("p h q -> p (h q)"), op=ALU.mult),
                     reads=[f"ET{e}_{i}", "BT"], writes=[f"ET{e}_{i}"])

    def E2(c):
        e = c % 2
        c4 = c % 4
        for kv in range(2):
            bank = "b6" if kv == 0 else "b7"
            for g in range(4):
                for j in (-1, 0, 1):
                    i = kv * 3 + j + 1
                    sl = (c + j) % 6
                    T.op("pe", lambda: nc.tensor.matmul(out=PS[bank][:, g * 65:(g + 1) * 65],
                                                        lhsT=ET[e][:, i, g * 128:(g + 1) * 128],
                                                        rhs=VA[:, sl, kv, :], start=(j == -1), stop=(j == 1)),
                         reads=[f"ET{e}_{i}", f"VA{sl}", f"VA{sl}o"], writes=[bank])
        for kv in range(2):
            bank = "b6" if kv == 0 else "b7"
            pv = PS[bank][:, 0:260].rearrange("p (g e) -> p g e", e=65)
            T.op("dve", lambda: nc.vector.tensor_tensor(out=den[e][:, 4 * kv:4 * kv + 4], in0=pv[:, :, 64],
                                                        in1=esink[:, 4 * kv:4 * kv + 4], op=ALU.add),
                 reads=[bank, "esink"], writes=[f"den{e}_{kv}"])
        T.op("dve", lambda: nc.vector.reciprocal(out=rden[e][:], in_=den[e][:]),
             reads=[f"den{e}_0", f"den{e}_1"], writes=[f"rden{e}"])
        for kv in range(2):
            bank = "b6" if kv == 0 else "b7"
            pv = PS[bank][:, 0:260].rearrange("p (g e) -> p g e", e=65)
            T.op("dve", lambda: nc.vector.tensor_tensor(
                out=araw[e][:, 256 * kv:256 * kv + 256].rearrange("p (g d) -> p g d", d=64), in0=pv[:, :, 0:64],
                in1=rden[e][:, 4 * kv:4 * kv + 4].unsqueeze(2).to_broadcast([128, 4, 64]), op=ALU.mult),
                 reads=[bank, f"rden{e}"], writes=[f"araw{e}_{kv}"])
        jk, jkey = junk_next()
        T.op("dve", lambda: nc.vector.scalar_tensor_tensor(out=jk[:, 0:512], in0=araw[e][:], scalar=1.0,
                                                           in1=araw[e][:], op0=ALU.mult, op1=ALU.mult,
                                                           accum_out=ssa[:, c4:c4 + 1]),
             reads=[f"araw{e}_0", f"araw{e}_1"], writes=[jkey, f"ssa{c4}"])
        rsqrt(ra[:, c4:c4 + 1], ssa[:, c4:c4 + 1], 1.0 / 512, f"ssa{c4}", f"ra{c4}", 1)
        T.op("pool", lambda: nc.gpsimd.tensor_scalar(out=amix[e][:], in0=araw[e][:], scalar1=ra[:, c4:c4 + 1],
                                                     scalar2=1.0, op0=ALU.mult, op1=ALU.mult),
             reads=[f"araw{e}_0", f"araw{e}_1", f"ra{c4}"], writes=[f"amix{e}"])

    def F1(c):
        e = c % 2
        gm = c % 6
        for i in range(8):
            src = amix[e][:, i * 128:(i + 1) * 128] if i < 4 else gmix[gm][:, (i - 4) * 128:(i - 3) * 128]
            T.op("pe", lambda: nc.tensor.transpose(out=TR[:, i * 128:(i + 1) * 128], in_=src, identity=identb[:]),
                 reads=[f"amix{e}", f"gmix{gm}", "identb"], writes=["tr"])
        T.op("dve", lambda: nc.vector.tensor_copy(out=mixT[e][:, 0:4, :], in_=TR[:, 0:512]), reads=["tr"],
             writes=[f"mixT{e}a"])
        T.op("dve", lambda: nc.vector.tensor_copy(out=mixT[e][:, 4:8, :], in_=TR[:, 512:1024]), reads=["tr"],
             writes=[f"mixT{e}b"])

    def XR_load(c):
        s, kb, main, m = ginfo(c)
        T.op("sp", lambda: nc.sync.dma_start(out=xr[m % 2][:], in_=xc[s]), writes=[f"xr{m % 2}"],
             slot=f"xr{m % 2}")

    def F2(c):
        s, kb, main, m = ginfo(c)
        e = c % 2
        r = m % 2
        for nh in range(2):
            bank = "b3" if nh == 0 else "b4"
            for kc in range(8):
                T.op("pe", lambda: nc.tensor.matmul(out=PS[bank][:, :], lhsT=mixT[e][:, kc, :],
                                                    rhs=Wo[:, kc, nh * 512:(nh + 1) * 512], start=(kc == 0),
                                                    stop=(kc == 7)),
                     reads=[f"mixT{e}a", f"mixT{e}b", f"Wo{kc}"], writes=[bank])
        for nh in range(2):
            bank = "b3" if nh == 0 else "b4"
            T.op("dve", lambda: nc.vector.tensor_tensor(out=xr[r][:, nh * 512:(nh + 1) * 512], in0=PS[bank][:, :],
                                                        in1=xr[r][:, nh * 512:(nh + 1) * 512], op=ALU.add),
                 reads=[bank, f"xr{r}"], writes=[f"xr{r}"])
        T.op("sp", lambda: nc.sync.dma_start(out=x1_d[m * 128:(m + 1) * 128, :], in_=xr[r][:]),
             reads=[f"xr{r}"], writes=[f"x1d{m}"], slot=f"xs{r}")

    wunits = []
    for (c0, c1) in ((0, 1024), (1024, 2048), (2048, DFF)):
        for wname in ("g", "u"):
            for kc in range(8):
                wunits.append((wname, kc, c0, c1))
    PIECES = ((0, 512), (512, 1024), (1024, 2048), (2048, DFF))
    for fc in range(NFC):
        wunits.append(("d", fc, 0, D))

    def W_load(u):
        wname, kc, c0, c1 = wunits[u]
        i = u % 4
        srcw = {"g": w_gate_d, "u": w_up_d, "d": w_down_d}[wname]
        n = c1 - c0
        T.op("sp", lambda: nc.sync.dma_start(out=wst[i][:, 0:n], in_=srcw[kc * 128:(kc + 1) * 128, c0:c1]),
             writes=[f"wst{i}"], slot=f"wst{i}")

    def W_cast(u):
        wname, kc, c0, c1 = wunits[u]
        i = u % 4
        j = u % 2
        n = c1 - c0
        sc = 1.0 if wname == "d" else g2t[:, kc:kc + 1]
        if u % 2 == 0:
            T.op("pool", lambda: nc.gpsimd.tensor_scalar(out=wcb[j][:, 0:n], in0=wst[i][:, 0:n], scalar1=sc,
                                                         scalar2=1.0, op0=ALU.mult, op1=ALU.mult),
                 reads=[f"wst{i}", "g2t"], writes=[f"wcb{j}"])
        else:
            T.op("act", lambda: nc.scalar.activation(out=wcb[j][:, 0:n], in_=wst[i][:, 0:n], func=AF.Copy, scale=sc),
                 reads=[f"wst{i}", "g2t"], writes=[f"wcb{j}"])

    def W_store(u):
        wname, kc, c0, c1 = wunits[u]
        j = u % 2
        dst = {"g": wg_s, "u": wu_s, "d": wd_s}[wname]
        n = c1 - c0
        T.op("sp", lambda: nc.sync.dma_start(out=dst[:, kc, c0:c1], in_=wcb[j][:, 0:n]),
             reads=[f"wcb{j}"], writes=[f"ws_{wname}_{kc}_{c0}"], slot=f"wcb{j}")

    def W_prefetch(kc):
        for wname, ws, pf in (("g", wg_s, PFg), ("u", wu_s, PFu)):
            T.op("sp", lambda: nc.sync.dma_start(out=pf[:, kc, :], in_=ws[:, kc, 0:512]),
                 reads=[f"ws_{wname}_{kc}_0"], writes=[f"W{wname}0"], slot=f"pf{wname}")

    def is_main(G):
        return 0 <= G < NG and LAYOUT[G] == "main"

    WLAG = 5
    ck('c4')
    for G in range(0, 3):
        if real(G):
            A_load(G)
    for G in range(0, 2):
        if real(G):
            A_comp(G)
    if real(0):
        B(0)
    for t in range(NG + 5 if DEBUG_STEPS is None else DEBUG_STEPS):
        if real(t + 3):
            A_load(t + 3)
        if is_main(t - 3):
            XR_load(t - 3)
        for uu in (2 * (t - WLAG), 2 * (t - WLAG) + 1):
            if 0 <= uu < len(wunits):
                W_load(uu)
        if is_main(t - 4):
            F1(t - 4)
        if real(t + 1):
            B(t + 1)
        if is_main(t - 2):
            E1(t - 2)
        if real(t):
            C(t)
        elif t < NG:
            Virt(t)
        if real(t + 2):
            A_comp(t + 2)
        if is_main(t - 1):
            D2(t - 1)
        if is_main(t - 3):
            E2(t - 3)
        if is_main(t - 4):
            F2(t - 4)
        if real(t):
            Dst(t)
        for uu in (2 * (t - WLAG - 2), 2 * (t - WLAG - 2) + 1):
            if 0 <= uu < len(wunits):
                W_store(uu)
        for uu in (2 * (t - WLAG - 1), 2 * (t - WLAG - 1) + 1):
            if 0 <= uu < len(wunits):
                W_cast(uu)
        if 0 <= t - (14 + WLAG) < 8:
            W_prefetch(t - (14 + WLAG))

    T.barrier()
    esA.close()
    stacks.pop()

    if not DEBUG_X1:
        Wgp = [PFg] + [sb(f"Wg{pi}", [128, 8, c1 - c0], BF16) for pi, (c0, c1) in enumerate(PIECES) if pi > 0]
        Wup = [PFu] + [sb(f"Wu{pi}", [128, 8, c1 - c0], BF16) for pi, (c0, c1) in enumerate(PIECES) if pi > 0]
        Wd = sb("Wd", [128, NFC, D], BF16)
        x1p = [sb(f"x1p{i}", [128, D], F32) for i in range(7)]
        h2 = [sb(f"h2_{i}", [128, D], BF16) for i in range(2)]
        h2T = sb("h2T", [128, 8, GRP * 128], BF16)
        actT = sb("actT", [128, NFC, GRP * 128], BF16)
        sg = [sb(f"sg{i}", [128, GRP * 128], F32) for i in range(2)]
        ss2 = sb("ss2", [128, 4], F32)
        rs2 = sb("rs2", [128, 4], F32)

        NGRP = NM // GRP
        banksB = ["b1", "b2", "b3", "b4", "b5", "b6", "b7"]
        bctr = [0]

        def nbank():
            b = banksB[bctr[0] % 7]
            bctr[0] += 1
            return b

        def P_load(m):
            r = m % 7
            T.op("sp", lambda: nc.sync.dma_start(out=x1p[r][:], in_=x1_d[m * 128:(m + 1) * 128, :]),
                 reads=[f"x1d{m}"], writes=[f"x1p{r}"], slot=f"x1p{r}")

        def P_norm(m):
            r = m % 7
            c = m % 4
            jk, jkey = junk_next()
            T.op("dve", lambda: nc.vector.scalar_tensor_tensor(out=jk[:], in0=x1p[r][:], scalar=1.0,
                                                               in1=x1p[r][:], op0=ALU.mult, op1=ALU.mult,
                                                               accum_out=ss2[:, c:c + 1]),
                 reads=[f"x1p{r}"], writes=[jkey, f"ss2_{c}"])
            rsqrt(rs2[:, c:c + 1], ss2[:, c:c + 1], 1.0 / D, f"ss2_{c}", f"rs2_{c}", 1)
            T.op("dve", lambda: nc.vector.tensor_scalar(out=h2[m % 2][:], in0=x1p[r][:], scalar1=rs2[:, c:c + 1],
                                                        scalar2=None, op0=ALU.mult),
                 reads=[f"x1p{r}", f"rs2_{c}"], writes=[f"h2_{m % 2}"])

        def P_tr(m):
            i4 = m % GRP
            for i in range(8):
                T.op("pe", lambda: nc.tensor.transpose(out=TR[:, i * 128:(i + 1) * 128],
                                                       in_=h2[m % 2][:, i * 128:(i + 1) * 128], identity=identb[:]),
                     reads=[f"h2_{m % 2}", "identb"], writes=["tr"])
            T.op("dve", lambda: nc.vector.tensor_copy(
                out=h2T[:, 0:4, i4 * 128:(i4 + 1) * 128],
                in_=TR[:, 0:512].rearrange("p (c t) -> p c t", t=128)), reads=["tr"], writes=["h2Ta"])
            T.op("dve", lambda: nc.vector.tensor_copy(
                out=h2T[:, 4:8, i4 * 128:(i4 + 1) * 128],
                in_=TR[:, 512:1024].rearrange("p (c t) -> p c t", t=128)), reads=["tr"], writes=["h2Tb"])

        def GU(g):
            for fc in range(NFC):
                pi = [p_ for p_, (c0, c1) in enumerate(PIECES) if c0 <= fc * 128 < c1][0]
                pc0 = PIECES[pi][0]
                bg = nbank()
                bu = nbank()
                for bank, wt, wname in ((bg, Wgp[pi], "Wg"), (bu, Wup[pi], "Wu")):
                    for kc in range(8):
                        T.op("pe", lambda: nc.tensor.matmul(out=PS[bank][:, :],
                                                            lhsT=wt[:, kc, fc * 128 - pc0:(fc + 1) * 128 - pc0],
                                                            rhs=h2T[:, kc, :], start=(kc == 0), stop=(kc == 7)),
                             reads=[f"{wname}{pi}", "h2Ta", "h2Tb"], writes=[bank])
                T.op("act", lambda: nc.scalar.activation(out=sg[fc % 2][:], in_=PS[bg][:, :], func=AF.Silu),
                     reads=[bg], writes=[f"sg{fc % 2}"])
                T.op("dve", lambda: nc.vector.tensor_tensor(out=actT[:, fc, :], in0=PS[bu][:, :], in1=sg[fc % 2][:],
                                                            op=ALU.mult),
                     reads=[bu, f"sg{fc % 2}"], writes=[f"actT{fc}"])

        def DN_tile(g, i4):
            m = g * GRP + i4
            r = m % 7
            bks = [nbank(), nbank()]
            for nh in range(2):
                for fc in range(NFC):
                    T.op("pe", lambda: nc.tensor.matmul(out=PS[bks[nh]][:, :],
                                                        lhsT=actT[:, fc, i4 * 128:(i4 + 1) * 128],
                                                        rhs=Wd[:, fc, nh * 512:(nh + 1) * 512], start=(fc == 0),
                                                        stop=(fc == NFC - 1)),
                         reads=[f"actT{fc}", f"Wd{fc // 11}"], writes=[bks[nh]])
            for nh in range(2):
                T.op("dve", lambda: nc.vector.tensor_tensor(out=x1p[r][:, nh * 512:(nh + 1) * 512],
                                                            in0=PS[bks[nh]][:, :],
                                                            in1=x1p[r][:, nh * 512:(nh + 1) * 512], op=ALU.add),
                     reads=[bks[nh], f"x1p{r}"], writes=[f"x1p{r}"])
            T.op("sp", lambda: nc.sync.dma_start(out=y_d[m * 128:(m + 1) * 128, :], in_=x1p[r][:]),
                 reads=[f"x1p{r}"], writes=[f"y{m}"], slot=f"ys{r}")

        for m in range(GRP):
            P_load(m)
        for pi, (c0, c1) in enumerate(PIECES):
            if pi == 0:
                continue
            for wname, ws, wt in (("g", wg_s, Wgp[pi]), ("u", wu_s, Wup[pi])):
                T.op("sp", lambda: nc.sync.dma_start(out=wt[:], in_=ws[:, :, c0:c1]),
                     writes=[f"W{wname}{pi}"], slot=f"W{wname}{pi}")
        for hh in range(2):
            T.op("sp", lambda: nc.sync.dma_start(out=Wd[:, hh * 11:(hh + 1) * 11, :], in_=wd_s[:, hh * 11:(hh + 1) * 11, :]),
                 writes=[f"Wd{hh}"], slot=f"Wd{hh}")

        for m in range(GRP):
            P_norm(m)
            P_tr(m)
        for g in range(NGRP):
            nxt = g + 1 < NGRP
            m0 = (g + 1) * GRP
            if nxt:
                for i4 in range(3):
                    P_load(m0 + i4)
            GU(g)
            if nxt:
                for i4 in range(3):
                    P_norm(m0 + i4)
                    P_tr(m0 + i4)
            for i4 in range(GRP):
                DN_tile(g, i4)
                if nxt and i4 == 0:
                    P_load(m0 + 3)
                    P_norm(m0 + 3)
                    P_tr(m0 + 3)

    return


def t5_buckets_np(rel):
    half = 16
    max_exact = 8
    ret = (rel > 0).astype(np.int32) * half
    n = np.abs(rel)
    large = max_exact + (np.log(np.maximum(n, 1).astype(np.float32) / max_exact)
                         / np.log(128 / max_exact) * (half - max_exact)).astype(np.int32)
    large = np.minimum(large, half - 1)
    return (ret + np.where(n < max_exact, n, large)).astype(np.int32)


_CACHE = {}


def kernel(x_prompt, x_sample, rel_bias_table, norm1, w_in, q_gain, k_gain, sink, v_gain, w_s, b_s,
           attn_out_gain, gmlp_out_gain, w_o, norm2, w_gate, w_up, w_down):
    f = np.float32
    x_prompt = np.asarray(x_prompt, f)
    x_sample = np.asarray(x_sample, f)
    seqs = [x_prompt[i] for i in range(x_prompt.shape[0])] + [x_sample[i] for i in range(x_sample.shape[0])]
    assert len(seqs) == 12

    w_in0 = np.asarray(w_in, f)[0]
    hperm = [0, 4, 1, 5, 2, 6, 3, 7]
    qcols = np.concatenate([np.arange(h * 64, (h + 1) * 64) for h in hperm])
    w_in_p = np.ascontiguousarray(np.concatenate([w_in0[:, qcols], w_in0[:, 512:]], axis=1))
    col = lambda v: np.ascontiguousarray(np.asarray(v, f).reshape(8, 128).T)
    g1 = col(np.asarray(norm1, f)[0])
    g2 = col(np.asarray(norm2, f)[0])
    gm = col(np.concatenate([np.asarray(attn_out_gain, f)[0], np.asarray(gmlp_out_gain, f)[0]]))
    qg = np.ascontiguousarray(np.tile(np.asarray(q_gain, f)[0], 2).reshape(128, 1))
    kg = np.ascontiguousarray(np.tile(np.asarray(k_gain, f)[0], 2).reshape(128, 1))
    vgb = np.ascontiguousarray(np.broadcast_to(np.asarray(v_gain, f)[0][None, :], (128, 512)))
    bsb = np.ascontiguousarray(np.repeat(np.asarray(b_s, f)[0].T[:, :, None], 64, axis=2).reshape(128, 512))
    wsT = np.ascontiguousarray(np.transpose(np.asarray(w_s, f)[0], (2, 0, 1)))
    tab = np.ascontiguousarray(np.asarray(rel_bias_table, f))
    sinkb = np.ascontiguousarray(np.broadcast_to(np.asarray(sink, f)[0][None, :], (128, 8)))
    ident = np.eye(128, dtype=f)
    oht = np.zeros((33, 512), f)
    for i in range(512):
        r = 255 - i
        if abs(r) <= 128:
            oht[int(t5_buckets_np(np.array([r]))[0]), i] = 1.0
        else:
            oht[32, i] = 1.0

    shared = dict(w_in=w_in_p, w_o=np.ascontiguousarray(np.asarray(w_o, f)[0]),
                  w_gate=np.ascontiguousarray(np.asarray(w_gate, f)[0]),
                  w_up=np.ascontiguousarray(np.asarray(w_up, f)[0]),
                  w_down=np.ascontiguousarray(np.asarray(w_down, f)[0]),
                  g1=g1, g2=g2, gm=gm, qg=qg, kg=kg, vgb=vgb, bsb=bsb, wsT=wsT, tab=tab, oht=oht,
                  sinkb=sinkb, ident=ident, jrev=np.ascontiguousarray(ident[::-1]))

    in_maps = []
    for c in range(8):
        xcore = np.zeros((NX, 128, D), f)
        flags = np.zeros((128, NF), f)
        hf = c % 2
        xs = seqs[8 + c // 2].reshape(32, 128, D)
        xcore[1:17] = xs[16 * hf:16 * hf + 16]
        if hf == 1:
            xcore[0] = xs[15]
            flags[:, 0] = 1.0
        else:
            xcore[17] = xs[16]
            flags[:, 1] = 1.0
        xcore[18:50] = seqs[c].reshape(32, 128, D)
        m = dict(shared)
        m["xc"] = xcore
        m["flags"] = flags
        in_maps.append(m)

    if "nc" not in _CACHE:
        _CACHE["nc"] = build_program()[0]
    nc = _CACHE["nc"]
    res = run_bass_kernel_spmd(nc, in_maps, core_ids=list(range(8)))
    key = "x1d" if DEBUG_X1 else "y"
    outs = [np.asarray(r[key]).reshape(NM * 128, D) for r in res.results]
    full = [np.zeros((4096, D), f) for _ in seqs]
    for c in range(8):
        hf = c % 2
        full[8 + c // 2][2048 * hf:2048 * hf + 2048] = outs[c][0:2048]
        full[c][:] = outs[c][2048:6144]
    yp = np.stack(full[:x_prompt.shape[0]]).astype(f)
    ys = np.stack(full[x_prompt.shape[0]:]).astype(f)
    return (yp, ys)
```

```python
import math
from contextlib import ExitStack

import numpy as np
import concourse.bass as bass
import concourse.mybir as mybir
from concourse.alu_op_type import AluOpType as ALU
from concourse.bass_utils import run_bass_kernel_spmd

F32 = mybir.dt.float32
BF16 = mybir.dt.bfloat16
AF = mybir.ActivationFunctionType
AX = mybir.AxisListType

D = 1024
LAYOUT = ["virt"] + ["main"] * 32 + ["virt", "halo"] + ["main"] * 16 + ["halo"]
NG = len(LAYOUT)
NM = 48
NX = 50
NF = 2
GINFO = []
_x = _m = _f = 0
for _k in LAYOUT:
    if _k == "virt":
        GINFO.append((None, _k, False, None, None))
    elif _k == "halo":
        GINFO.append((_x, _k, False, None, _f))
        _x += 1
        _f += 1
    else:
        GINFO.append((_x, _k, True, _m, None))
        _x += 1
        _m += 1
DFF = 2816
NFC = DFF // 128
EPS = 1e-6
GRP = 4
DEBUG_X1 = False
DEBUG_STEPS = None
DEBUG_STOP = None


class _Stop(Exception):
    pass


PSUM_KEYS = {"tr", "b1", "b2", "b3", "b4", "b5", "b6", "b7"}


STALL_LOG = None
XLAT = 0.35
XLAT_PE = 1.5


class Deferred:
    def __init__(self, meth, eng, name, args, kwargs):
        self.meth, self.eng, self.name, self.args, self.kwargs = meth, eng, name, args, kwargs

    def emit(self):
        return self.meth(*self.args, **self.kwargs)


class EngProxy:
    def __init__(self, real, eng):
        self._real, self._eng = real, eng

    def __getattr__(self, m):
        real_m = getattr(self._real, m)
        eng = self._eng

        def f(*a, **k):
            return Deferred(real_m, eng, m, a, k)
        return f


class NCProxy:
    def __init__(self, nc):
        self._nc = nc
        self.tensor = EngProxy(nc.tensor, "pe")
        self.vector = EngProxy(nc.vector, "dve")
        self.scalar = EngProxy(nc.scalar, "act")
        self.gpsimd = EngProxy(nc.gpsimd, "pool")
        self.sync = EngProxy(nc.sync, "sp")

    def __getattr__(self, a):
        return getattr(self._nc, a)


def _fsize(ap):
    n = 1
    for d in ap.shape[1:]:
        n *= d
    return n


def est_cost(d):
    k = d.kwargs
    if d.eng == "pe":
        if d.name == "transpose":
            return 0.08
        return max(0.036, _fsize(k["rhs"]) / 2400.0 + 0.012)
    if d.eng == "act":
        n = _fsize(k["out"])
        return 0.18 + n / 1250.0 + (0.09 if k.get("accum_out") is not None else 0.0)
    if d.eng == "dve":
        out = k["out"] if "out" in k else d.args[0]
        n = _fsize(out)
        c = 0.15 + n / 960.0
        if d.name in ("tensor_copy", "tensor_scalar"):
            c = 0.15 + n / 1920.0
        if d.name == "reciprocal":
            c = 0.2
        if d.name == "tensor_reduce":
            c = 0.15 + _fsize(k["in_"]) / 960.0
        if k.get("accum_out") is not None:
            c += 0.02
        return c
    if d.eng == "pool":
        out = k["out"] if "out" in k else d.args[0]
        n = _fsize(out)
        if d.name == "tensor_tensor":
            return 0.1 + n * 0.002
        return 0.25 + n * 0.0006
    return 0.06


def act_set(d):
    if d.eng != "act":
        return None
    f = d.kwargs.get("func")
    if f in (AF.Exp, AF.Ln):
        return "exp"
    if f == AF.Gelu_apprx_tanh:
        return "gelu"
    if f == AF.Silu:
        return "silu"
    return None


class Tracker:
    def __init__(self, nc, es):
        self.nc = nc
        self.es = es
        self.eng = {"pe": nc.tensor, "act": nc.scalar, "dve": nc.vector, "pool": nc.gpsimd, "sp": nc.sync}
        self.handle = {}
        self.count = {}
        for k in ("pe", "act", "dve", "pool"):
            self.handle["E:" + k] = es.enter_context(nc.semaphore("sem_" + k))
            self.count["E:" + k] = 0
        self.clock = {k: {} for k in self.eng}
        self.snap = {}
        self.last_w = {}
        self.readers = {}
        self.last_acc = {}
        self.rec = []
        self.sim_time = 0.0
        self.sim_log = []
        self.nwaits = 0
        self.nops = 0

    def _slot(self, slot):
        sid = "D:" + slot
        if sid not in self.handle:
            self.handle[sid] = self.es.enter_context(self.nc.semaphore("dsem_" + slot))
            self.count[sid] = 0
        return sid

    def op(self, eng, fn, reads=(), writes=(), slot=None):
        d = fn()
        self.rec.append((eng, d, tuple(reads), tuple(writes), slot))

    def flush(self):
        recs = self.rec
        self.rec = []
        n = len(recs)
        if n == 0:
            return
        deps = [None] * n
        lw, rd, la = {}, {}, {}
        for i, (eng, d, reads, writes, slot) in enumerate(recs):
            s = set()
            for k in reads + writes:
                if k in PSUM_KEYS:
                    if k in la:
                        s.add(la[k])
                    la[k] = i
            for k in reads:
                if k in PSUM_KEYS:
                    continue
                if k in lw:
                    s.add(lw[k])
            for k in writes:
                if k in PSUM_KEYS:
                    continue
                if k in lw:
                    s.add(lw[k])
                for j in rd.get(k, ()):
                    s.add(j)
            for k in writes:
                if k not in PSUM_KEYS:
                    lw[k] = i
                    rd[k] = []
            for k in reads:
                if k not in PSUM_KEYS:
                    rd.setdefault(k, []).append(i)
            s.discard(i)
            deps[i] = s
        users = [[] for _ in range(n)]
        ndep = [0] * n
        for i in range(n):
            ndep[i] = len(deps[i])
            for j in deps[i]:
                users[j].append(i)
        cost = [est_cost(r[1]) for r in recs]
        aset = [act_set(r[1]) for r in recs]
        engs = ("pe", "act", "dve", "pool", "sp")
        elig = {e: [] for e in engs}
        for i in range(n):
            if ndep[i] == 0:
                elig[recs[i][0]].append(i)
        free_at = {e: 0.0 for e in engs}
        cur_set = [None]
        dma_busy = [0.0]
        finish = [0.0] * n
        ready = [0.0] * n
        done = [False] * n
        order = []
        lo = 0
        WINDOW = 700
        nsched = 0
        while nsched < n:
            while lo < n and done[lo]:
                lo += 1
            best = None
            for e in engs:
                fa = free_at[e]
                for i in elig[e]:
                    if i > lo + WINDOW:
                        continue
                    st = ready[i] if ready[i] > fa else fa
                    if e == "act" and aset[i] is not None and aset[i] != cur_set[0]:
                        st += 1.3
                    key = (round(st / 0.25), i)
                    if best is None or key < best[0]:
                        best = (key, i, e, st)
            if best is None:
                for e in engs:
                    for i in elig[e]:
                        st = max(ready[i], free_at[e])
                        key = (i,)
                        if best is None or key < best[0]:
                            best = (key, i, e, st)
            _, i, e, st = best
            elig[e].remove(i)
            if STALL_LOG is not None and st > free_at[e] + 0.01 and deps[i]:
                j = max(deps[i], key=lambda q: finish[q])
                STALL_LOG.append((e, st - free_at[e], recs[j][0], recs[j][1].name, recs[j][3], recs[i][1].name,
                                  recs[i][2], recs[i][3], st))
            if e == "act" and aset[i] is not None:
                cur_set[0] = aset[i]
            if e == "sp":
                free_at[e] = st + 0.06
                out = recs[i][1].kwargs.get("out")
                esz = 2 if (out is not None and out.dtype == BF16) else 4
                nbytes = 128 * _fsize(out) * esz if out is not None else 0
                tx0 = max(st, dma_busy[0])
                dma_busy[0] = tx0 + nbytes / 1.9e5
                finish[i] = dma_busy[0] + 2.0
            else:
                free_at[e] = st + cost[i]
                finish[i] = st + cost[i]
            done[i] = True
            nsched += 1
            order.append((st, i))
            for u in users[i]:
                ndep[u] -= 1
                lat = (0.0 if e == "pe" else 0.08) if recs[u][0] == e else (XLAT_PE if recs[u][0] == "pe" else XLAT)
                if finish[i] + lat > ready[u]:
                    ready[u] = finish[i] + lat
                if ndep[u] == 0:
                    elig[recs[u][0]].append(u)
        order.sort()
        self.sim_time = max(finish)
        self.sim_log.append((n, self.sim_time))
        for st, i in order:
            eng, d, reads, writes, slot = recs[i]
            self._emit(eng, d, reads, writes, slot)

    def _emit(self, eng, dfr, reads=(), writes=(), slot=None):
        fn = dfr.emit
        need = {}

        def add(d):
            if d is not None and need.get(d[0], 0) < d[1]:
                need[d[0]] = d[1]

        excl = [k for k in list(reads) + list(writes) if k in PSUM_KEYS]
        reads = [k for k in reads if k not in PSUM_KEYS]
        writes = [k for k in writes if k not in PSUM_KEYS]
        for k in excl:
            d = self.last_acc.get(k)
            if d is not None and d[0] != "E:" + eng:
                add(d)
        for k in reads:
            add(self.last_w.get(k))
        for k in writes:
            add(self.last_w.get(k))
            for s, v in self.readers.get(k, {}).items():
                add((s, v))
        cl = self.clock[eng]
        waits = []
        for s, v in need.items():
            if eng == "pe" and s == "E:pe":
                continue
            if cl.get(s, 0) < v:
                waits.append((s, v))
        for s, v in waits:
            sn = self.snap.get((s, v))
            if sn is not None:
                for s2, v2 in sn.items():
                    if cl.get(s2, 0) < v2:
                        cl[s2] = v2
            if cl.get(s, 0) < v:
                cl[s] = v
        e = self.eng[eng]
        for s, v in waits[:-1]:
            e.wait_ge(self.handle[s], v)
        inst = fn()
        if waits:
            inst._wait_ge(self.handle[waits[-1][0]], waits[-1][1])
        self.nwaits += len(waits)
        self.nops += 1
        if slot is not None:
            sid = self._slot(slot)
            self.count[sid] += 1
            val = 16 * self.count[sid]
            inst.then_inc(self.handle[sid], 16)
        else:
            sid = "E:" + eng
            self.count[sid] += 1
            val = self.count[sid]
            inst.then_inc(self.handle[sid], 1)
        sn = dict(cl)
        sn[sid] = val
        self.snap[(sid, val)] = sn
        me = (sid, val)
        for k in excl:
            self.last_acc[k] = me
        for k in writes:
            self.last_w[k] = me
            self.readers[k] = {}
        for k in reads:
            r = self.readers.setdefault(k, {})
            if r.get(sid, 0) < val:
                r[sid] = val
        return inst

    def barrier(self):
        self.flush()
        for en, e in self.eng.items():
            cl = self.clock[en]
            for sid, c in self.count.items():
                v = c * 16 if sid.startswith("D:") else c
                if v > 0 and cl.get(sid, 0) < v:
                    e.wait_ge(self.handle[sid], v)
                    cl[sid] = v


def build_program():
    nc = bass.Bass("TRN2", target_bir_lowering=False)
    es = ExitStack()
    T = Tracker(nc, es)
    stacks = []
    try:
        _build_body(NCProxy(nc), es, T, stacks)
    except _Stop:
        pass
    T.barrier()
    for st in reversed(stacks):
        st.close()
    es.close()
    return nc, T


def _build_body(nc, es, T, stacks):
    def ck(name):
        if DEBUG_STOP == name:
            raise _Stop()


    def din(name, shape):
        return nc.dram_tensor(name, list(shape), F32, kind="ExternalInput").ap()

    xc = din("xc", [NX, 128, D])
    flags_d = din("flags", [128, NF])
    w_in_d = din("w_in", [D, 1792])
    w_o_d = din("w_o", [D, D])
    w_gate_d = din("w_gate", [D, DFF])
    w_up_d = din("w_up", [D, DFF])
    w_down_d = din("w_down", [DFF, D])
    g1_d = din("g1", [128, 8])
    g2_d = din("g2", [128, 8])
    gm_d = din("gm", [128, 8])
    qg_d = din("qg", [128, 1])
    kg_d = din("kg", [128, 1])
    vgb_d = din("vgb", [128, 512])
    bsb_d = din("bsb", [128, 512])
    wsT_d = din("wsT", [128, 8, 128])
    tab_d = din("tab", [32, 8])
    oht_d = din("oht", [33, 512])
    sinkb_d = din("sinkb", [128, 8])
    ident_d = din("ident", [128, 128])
    jrev_d = din("jrev", [128, 128])

    y_d = nc.dram_tensor("y", [NM * 128, D], F32, kind="ExternalOutput").ap()
    x1_h = nc.dram_tensor("x1d", [NM * 128, D], F32, kind="ExternalOutput" if DEBUG_X1 else "Internal")
    x1_d = x1_h.ap()
    txr_h = nc.dram_tensor("txr_d", [8, 512], F32)
    wg_s = nc.dram_tensor("wg_s", [128, 8, DFF], BF16).ap()
    wu_s = nc.dram_tensor("wu_s", [128, 8, DFF], BF16).ap()
    wd_s = nc.dram_tensor("wd_s", [128, NFC, D], BF16).ap()
    txr_d = txr_h.ap()

    def sb(name, shape, dt):
        return es.enter_context(nc.sbuf_tensor("s_" + name, list(shape), dt))

    identb = sb("identb", [128, 128], BF16)
    flags = sb("flags_sb", [128, NF], F32)
    flags2 = sb("flags2", [128, NF, 2], F32)
    ones2 = sb("ones2", [128, 2], F32)
    nhalf = sb("nhalf", [128, 16], F32)
    g2t = sb("g2t", [128, 8], F32)
    PFg = sb("PFg", [128, 8, 512], BF16)
    PFu = sb("PFu", [128, 8, 512], BF16)
    TRt = es.enter_context(nc.psum_tensor("tr", [128, 1024], BF16))
    PSt = es.enter_context(nc.psum_tensor("ps", [128, 7, 512], F32))
    TR = TRt
    PS = {f"b{i + 1}": PSt[:, i, :] for i in range(7)}
    ptmp = sb("ptmp", [128, 8, 16], F32)
    junkds = [sb(f"junkd{i}", [128, 1024], BF16) for i in range(3)]
    jctr = [0]
    junka = sb("junka", [128, 1024], BF16)

    def junk_next():
        i = jctr[0] % 3
        jctr[0] += 1
        return junkds[i], f"junkd{i}"
    pt_ctr = [0]

    def rsqrt(out, in_, mult, kin, kout, width):
        i = pt_ctr[0] % 8
        pt_ctr[0] += 1
        tmp = ptmp[:, i, 0:width]
        T.op("act", lambda: nc.scalar.activation(out=tmp, in_=in_, func=AF.Ln, scale=float(mult), bias=EPS),
             reads=[kin], writes=[f"ptmp{i}"])
        T.op("act", lambda: nc.scalar.activation(out=out, in_=tmp, func=AF.Exp, scale=-0.5),
             reads=[f"ptmp{i}"], writes=[kout])

    def load(dst, src, key, eng="sp"):
        q = nc.sync if eng == "sp" else nc.scalar
        T.op(eng, lambda: q.dma_start(out=dst, in_=src), writes=[key], slot=key)

    T.op("dve", lambda: nc.vector.memset(nhalf[:], -0.5), writes=["nhalf"])
    T.op("dve", lambda: nc.vector.memset(ones2[:], 1.0), writes=["ones2"])
    load(flags[:], flags_d[:, :], "flags")
    load(g2t[:], g2_d[:, :], "g2t")
    for i in range(2):
        T.op("dve", lambda: nc.vector.tensor_copy(out=flags2[:, :, i], in_=flags[:]), reads=["flags"],
             writes=["flags2"])

    ck('c0')
    esA = ExitStack()
    stacks.append(esA)

    def sbA(name, shape, dt):
        return esA.enter_context(nc.sbuf_tensor("a_" + name, list(shape), dt))

    Wi = sbA("Wi", [128, 8, 1792], BF16)
    Wo = sbA("Wo", [128, 8, 1024], BF16)
    wst = [sbA(f"wst{i}", [128, 1024], F32) for i in range(4)]
    wcb = [sbA(f"wcb{i}", [128, 1024], BF16) for i in range(2)]
    g1t = sbA("g1t", [128, 8], F32)
    gmt = sbA("gmt", [128, 8], F32)
    qgt = sbA("qgt", [128, 1], F32)
    kgt = sbA("kgt", [128, 1], F32)
    qkg = sbA("qkg", [128, 1], F32)
    vgb = sbA("vgb", [128, 512], F32)
    bsb = sbA("bsb", [128, 512], F32)
    wsf = sbA("wsf", [128, 8, 128], F32)
    wsb = sbA("wsb", [128, 8, 128], BF16)
    tab33 = sbA("tab33", [64, 8], F32)
    oht = sbA("oht", [64, 512], F32)
    txr = sbA("txr", [8, 512], F32)
    btf = sbA("btf", [128, 8, 128], F32)
    BT = sbA("BT", [128, 3, 8, 128], BF16)
    sinkb = sbA("sinkb", [128, 8], F32)
    esink = sbA("esink", [128, 8], F32)
    identf = sbA("identf", [128, 128], F32)
    jrevf = sbA("jrevf", [128, 128], F32)
    jrevb = sbA("jrevb", [128, 128], BF16)
    btb = sbA("btb", [128, 1024], BF16)

    xf = [sbA(f"xf{i}", [128, D], F32) for i in range(3)]
    hb = [sbA(f"h{i}", [128, D], BF16) for i in range(2)]
    hT = [sbA(f"hT{i}", [128, 8, 128], BF16) for i in range(2)]
    ss1 = sbA("ss1", [128, 4], F32)
    rs1 = sbA("rs1", [128, 4], F32)
    sq = [sbA(f"sq{i}", [128, 640], F32) for i in range(2)]
    ssq = [sbA(f"ssq{i}", [128, 10], F32) for i in range(2)]
    rqk = [sbA(f"rqk{i}", [128, 10], F32) for i in range(2)]
    qn = [sbA(f"qn{i}", [128, 512], BF16) for i in range(2)]
    kn = [sbA(f"kn{i}", [128, 128], BF16) for i in range(2)]
    QT = [sbA(f"QT{i}", [128, 4, 128], BF16) for i in range(3)]
    KT = sbA("KT", [128, 6, 128], BF16)
    VA = sbA("VA", [128, 6, 2, 65], BF16)
    gu = [sbA(f"gu{i}", [128, 512], F32) for i in range(2)]
    gv = [sbA(f"gv{i}", [128, 512], F32) for i in range(2)]
    vn = [sbA(f"vn{i}", [128, 512], BF16) for i in range(2)]
    graw = [sbA(f"graw{i}", [128, 512], F32) for i in range(2)]
    gmix = [sbA(f"gmix{i}", [128, 512], BF16) for i in range(6)]
    ssv = sbA("ssv", [128, 4], F32)
    rv = sbA("rv", [128, 4], F32)
    ssg = sbA("ssg", [128, 4], F32)
    rg = sbA("rg", [128, 4], F32)
    ET = [sbA(f"ET{i}", [128, 6, 512], BF16) for i in range(2)]
    den = [sbA(f"den{i}", [128, 8], F32) for i in range(2)]
    rden = [sbA(f"rden{i}", [128, 8], F32) for i in range(2)]
    araw = [sbA(f"araw{i}", [128, 512], F32) for i in range(2)]
    ssa = sbA("ssa", [128, 4], F32)
    ra = sbA("ra", [128, 4], F32)
    amix = [sbA(f"amix{i}", [128, 512], BF16) for i in range(2)]
    mixT = [sbA(f"mixT{i}", [128, 8, 128], BF16) for i in range(2)]
    xr = [sbA(f"xr{i}", [128, D], F32) for i in range(3)]

    load(g1t[:], g1_d[:, :], "g1t")
    cast_i = [0]

    def cast_fold(dst, src, gcol, rkeys, wkeys):
        i = cast_i[0] % 3
        cast_i[0] += 1
        if i == 0:
            T.op("dve", lambda: nc.vector.tensor_scalar(out=dst, in0=src, scalar1=gcol, scalar2=None,
                                                        op0=ALU.mult), reads=rkeys, writes=wkeys)
        elif i == 1:
            T.op("act", lambda: nc.scalar.activation(out=dst, in_=src, func=AF.Copy, scale=gcol),
                 reads=rkeys, writes=wkeys)
        else:
            T.op("pool", lambda: nc.gpsimd.tensor_scalar(out=dst, in0=src, scalar1=gcol, scalar2=1.0,
                                                         op0=ALU.mult, op1=ALU.mult), reads=rkeys, writes=wkeys)

    si = 0
    for kc in range(8):
        for (c0, c1) in ((0, 1024), (1024, 1792)):
            s_ = wst[si % 4]
            key = f"wst{si % 4}"
            si += 1
            load(s_[:, 0:c1 - c0], w_in_d[kc * 128:(kc + 1) * 128, c0:c1], key)
            cast_fold(Wi[:, kc, c0:c1], s_[:, 0:c1 - c0], g1t[:, kc:kc + 1], [key, "g1t"], [f"Wi{kc}"])
    load(identf[:], ident_d[:, :], "identf")
    T.op("dve", lambda: nc.vector.tensor_copy(out=identb[:], in_=identf[:]), reads=["identf"], writes=["identb"])
    load(gmt[:], gm_d[:, :], "gmt")
    load(qgt[:], qg_d[:, :], "qgt")
    load(kgt[:], kg_d[:, :], "kgt")
    T.op("dve", lambda: nc.vector.tensor_tensor(out=qkg[:], in0=qgt[:], in1=kgt[:], op=ALU.mult),
         reads=["qgt", "kgt"], writes=["qkg"])
    load(vgb[:], vgb_d[:, :], "vgb")
    load(bsb[:], bsb_d[:, :], "bsb")
    load(wsf[:], wsT_d[:, :, :], "wsf")
    T.op("dve", lambda: nc.vector.tensor_copy(out=wsb[:], in_=wsf[:]), reads=["wsf"], writes=["wsb"])
    load(sinkb[:], sinkb_d[:, :], "sinkb")
    T.op("act", lambda: nc.scalar.activation(out=esink[:], in_=sinkb[:], func=AF.Exp), reads=["sinkb"],
         writes=["esink"])

    ck('c1')
    T.op("dve", lambda: nc.vector.memset(tab33[:], -30000.0), writes=["tab33"])
    load(tab33[0:32, :], tab_d[:, :], "tab33")
    T.op("dve", lambda: nc.vector.memset(oht[:], 0.0), writes=["oht"])
    load(oht[0:33, :], oht_d[:, :], "oht")
    T.op("pe", lambda: nc.tensor.matmul(out=PS["b1"][0:8, 0:512], lhsT=tab33[0:64, :], rhs=oht[0:64, :],
                                        start=True, stop=True),
         reads=["tab33", "oht"], writes=["b1"])
    T.op("act", lambda: nc.scalar.activation(out=txr[:], in_=PS["b1"][0:8, 0:512], func=AF.Exp), reads=["b1"],
         writes=["txr"])
    T.op("sp", lambda: nc.sync.dma_start(out=txr_d[:, :], in_=txr[:]), reads=["txr"], writes=["txr_d"],
         slot="txr_st")
    load(jrevf[:], jrev_d[:, :], "jrevf")
    T.op("dve", lambda: nc.vector.tensor_copy(out=jrevb[:], in_=jrevf[:]), reads=["jrevf"], writes=["jrevb"])
    for j in range(3):
        src = bass.AP(txr_h, 128 - 128 * (j - 1), [[1, 128], [512, 8], [1, 128]])
        T.op("sp", lambda: nc.sync.dma_start(out=btf[:], in_=src), reads=["txr_d"], writes=["btf"], slot="btf")
        T.op("dve", lambda: nc.vector.tensor_copy(out=btb[:], in_=btf[:].rearrange("p h q -> p (h q)")),
             reads=["btf"], writes=["btb"])
        for hh in range(2):
            bank = "b2" if hh == 0 else "b3"
            T.op("pe", lambda: nc.tensor.matmul(out=PS[bank][:, :], lhsT=jrevb[:], rhs=btb[:, hh * 512:(hh + 1) * 512],
                                                start=True, stop=True), reads=["jrevb", "btb"], writes=[bank])
            T.op("dve", lambda: nc.vector.tensor_copy(
                out=BT[:, j, 4 * hh:4 * hh + 4, :].rearrange("p h q -> p (h q)"), in_=PS[bank][:, :]),
                 reads=[bank], writes=["BT"])

    ck('c2')
    for kc in range(8):
        s_ = wst[si % 4]
        key = f"wst{si % 4}"
        si += 1
        load(s_[:, 0:1024], w_o_d[kc * 128:(kc + 1) * 128, :], key)
        cast_fold(Wo[:, kc, :], s_[:, 0:1024], gmt[:, kc:kc + 1], [key, "gmt"], [f"Wo{kc}"])
    ck('c3')

    def ginfo(G):
        xi, kind, main, m, fidx = GINFO[G]
        return xi, fidx, main, m

    def real(G):
        return 0 <= G < NG and LAYOUT[G] != "virt"

    def Virt(G):
        slot = G % 6
        T.op("pool", lambda: nc.gpsimd.memset(VA[:, slot, :, :], 0.0), writes=[f"VA{slot}", f"VA{slot}o"])
        T.op("pool", lambda: nc.gpsimd.memset(KT[:, slot, :], 0.0), writes=[f"KT{slot}"])

    def A_load(G):
        s, kb, main, m = ginfo(G)
        r = G % 3
        T.op("sp", lambda: nc.sync.dma_start(out=xf[r][:], in_=xc[s]), writes=[f"xf{r}"], slot=f"xf{r}")

    def A_comp(G):
        r = G % 3
        c = G % 4
        T.op("act", lambda: nc.scalar.activation(out=junka[:], in_=xf[r][:], func=AF.Square,
                                                 accum_out=ss1[:, c:c + 1]),
             reads=[f"xf{r}"], writes=["junka", f"ss1_{c}"])
        rsqrt(rs1[:, c:c + 1], ss1[:, c:c + 1], 1.0 / D, f"ss1_{c}", f"rs1_{c}", 1)
        T.op("dve", lambda: nc.vector.tensor_scalar(out=hb[G % 2][:], in0=xf[r][:], scalar1=rs1[:, c:c + 1],
                                                    scalar2=None, op0=ALU.mult),
             reads=[f"xf{r}", f"rs1_{c}"], writes=[f"h{G % 2}"])

    def B(G):
        p = G % 2
        for i in range(8):
            T.op("pe", lambda: nc.tensor.transpose(out=TR[:, i * 128:(i + 1) * 128],
                                                   in_=hb[p][:, i * 128:(i + 1) * 128], identity=identb[:]),
                 reads=[f"h{p}", "identb"], writes=["tr"])
        ck('p3')
        T.op("dve", lambda: nc.vector.tensor_copy(out=hT[p][:, 0:4, :], in_=TR[:, 0:512]), reads=["tr"],
             writes=[f"hT{p}a"])
        ck('p4')
        T.op("dve", lambda: nc.vector.tensor_copy(out=hT[p][:, 4:8, :], in_=TR[:, 512:1024]), reads=["tr"],
             writes=[f"hT{p}b"])

    def C(G):
        s, kb, main, m = ginfo(G)
        p = G % 2
        c4 = G % 4
        slot = G % 6
        groups = [("b1", 0, 512), ("b2", 512, 768), ("b3", 768, 1280), ("b4", 1280, 1792)] if main else \
            [("b2", 512, 768)]
        for bank, c0, c1 in groups:
            for kc in range(8):
                T.op("pe", lambda: nc.tensor.matmul(out=PS[bank][:, 0:c1 - c0], lhsT=hT[p][:, kc, :],
                                                    rhs=Wi[:, kc, c0:c1], start=(kc == 0), stop=(kc == 7)),
                     reads=[f"hT{p}a", f"hT{p}b", f"Wi{kc}"], writes=[bank])
        if main:
            T.op("act", lambda: nc.scalar.activation(out=sq[p][:, 0:512], in_=PS["b1"][:, 0:512], func=AF.Square),
                 reads=["b1"], writes=[f"sq{p}a"])
        T.op("act", lambda: nc.scalar.activation(out=sq[p][:, 512:640], in_=PS["b2"][:, 0:128], func=AF.Square),
             reads=["b2"], writes=[f"sq{p}b"])
        h0 = 0 if main else 8
        nh = 10 - h0
        T.op("dve", lambda: nc.vector.tensor_reduce(
            out=ssq[p][:, h0:10], in_=sq[p][:, h0 * 64:640].rearrange("p (h d) -> p h d", d=64),
            axis=AX.X, op=ALU.add), reads=[f"sq{p}a", f"sq{p}b"], writes=[f"ssq{p}"])
        rsqrt(rqk[p][:, h0:10], ssq[p][:, h0:10], 1.0 / 64, f"ssq{p}", f"rqk{p}", nh)
        if main:
            T.op("dve", lambda: nc.vector.tensor_tensor(
                out=qn[p][:].rearrange("p (h d) -> p h d", d=64),
                in0=PS["b1"][:, 0:512].rearrange("p (h d) -> p h d", d=64),
                in1=rqk[p][:, 0:8].unsqueeze(2).to_broadcast([128, 8, 64]), op=ALU.mult),
                 reads=["b1", f"rqk{p}"], writes=[f"qn{p}"])
        T.op("dve", lambda: nc.vector.tensor_tensor(
            out=kn[p][:].rearrange("p (h d) -> p h d", d=64),
            in0=PS["b2"][:, 0:128].rearrange("p (h d) -> p h d", d=64),
            in1=rqk[p][:, 8:10].unsqueeze(2).to_broadcast([128, 2, 64]), op=ALU.mult),
             reads=["b2", f"rqk{p}"], writes=[f"kn{p}"])
        vsrc = PS["b2"][:, 128:256].rearrange("p (k d) -> p k d", d=64)
        if main:
            T.op("act", lambda: nc.scalar.activation(out=VA[:, slot, :, 0:64], in_=vsrc, func=AF.Copy), reads=["b2"],
                 writes=[f"VA{slot}"])
            T.op("pool", lambda: nc.gpsimd.tensor_copy(out=VA[:, slot, :, 64], in_=ones2[:]), reads=["ones2"],
                 writes=[f"VA{slot}o"])
        else:
            fi = kb
            T.op("act", lambda: nc.scalar.activation(out=VA[:, slot, :, 0:64], in_=vsrc, func=AF.Copy,
                                                     scale=flags[:, fi:fi + 1]),
                 reads=["b2", "flags"], writes=[f"VA{slot}"])
            T.op("pool", lambda: nc.gpsimd.tensor_copy(out=VA[:, slot, :, 64], in_=flags2[:, fi, :]),
                 reads=["flags2"], writes=[f"VA{slot}o"])
        if main:
            T.op("act", lambda: nc.scalar.activation(out=gu[p][:], in_=PS["b3"][:, :], func=AF.Gelu_apprx_tanh),
                 reads=["b3"], writes=[f"gu{p}"])
            T.op("act", lambda: nc.scalar.activation(out=gv[p][:], in_=PS["b4"][:, :], func=AF.Gelu_apprx_tanh),
                 reads=["b4"], writes=[f"gv{p}"])
            jk, jkey = junk_next()
            T.op("dve", lambda: nc.vector.scalar_tensor_tensor(out=jk[:, 0:512], in0=gv[p][:], scalar=1.0,
                                                               in1=gv[p][:], op0=ALU.mult, op1=ALU.mult,
                                                               accum_out=ssv[:, c4:c4 + 1]),
                 reads=[f"gv{p}"], writes=[jkey, f"ssv{c4}"])
            rsqrt(rv[:, c4:c4 + 1], ssv[:, c4:c4 + 1], 1.0 / 512, f"ssv{c4}", f"rv{c4}", 1)
            T.op("dve", lambda: nc.vector.scalar_tensor_tensor(out=vn[p][:], in0=gv[p][:], scalar=rv[:, c4:c4 + 1],
                                                               in1=vgb[:], op0=ALU.mult, op1=ALU.mult),
                 reads=[f"gv{p}", f"rv{c4}", "vgb"], writes=[f"vn{p}"])

    def Dst(G):
        s, kb, main, m = ginfo(G)
        p = G % 2
        c4 = G % 4
        slot = G % 6
        if main:
            for i in range(4):
                T.op("pe", lambda: nc.tensor.transpose(out=TR[:, i * 128:(i + 1) * 128],
                                                       in_=qn[p][:, i * 128:(i + 1) * 128], identity=identb[:]),
                     reads=[f"qn{p}", "identb"], writes=["tr"])
        T.op("pe", lambda: nc.tensor.transpose(out=TR[:, 512:640], in_=kn[p][:], identity=identb[:]),
             reads=[f"kn{p}", "identb"], writes=["tr"])
        if main:
            T.op("dve", lambda: nc.vector.tensor_copy(out=QT[G % 3][:].rearrange("p c t -> p (c t)"),
                                                      in_=TR[:, 0:512]), reads=["tr"], writes=[f"QT{G % 3}"])
        T.op("dve", lambda: nc.vector.tensor_scalar(out=KT[:, slot, :], in0=TR[:, 512:640], scalar1=qkg[:, 0:1],
                                                    scalar2=None, op0=ALU.mult),
             reads=["tr", "qkg"], writes=[f"KT{slot}"])

    def D2(G):
        s, kb, main, m = ginfo(G)
        p = G % 2
        c4 = G % 4
        if main:
            for h in range(8):
                T.op("pe", lambda: nc.tensor.matmul(out=PS["b5"][:, h * 64:(h + 1) * 64], lhsT=wsb[:, h, :],
                                                    rhs=vn[p][:, h * 64:(h + 1) * 64], start=True, stop=True),
                     reads=["wsb", f"vn{p}"], writes=["b5"])
            T.op("dve", lambda: nc.vector.tensor_tensor(out=graw[p][:], in0=PS["b5"][:, :], in1=bsb[:], op=ALU.add),
                 reads=["b5", "bsb"], writes=[f"graw{p}"])
            T.op("pool", lambda: nc.gpsimd.tensor_tensor(out=graw[p][:], in0=graw[p][:], in1=gu[p][:], op=ALU.mult),
                 reads=[f"graw{p}", f"gu{p}"], writes=[f"graw{p}"])
            jk, jkey = junk_next()
            T.op("dve", lambda: nc.vector.scalar_tensor_tensor(out=jk[:, 0:512], in0=graw[p][:], scalar=1.0,
                                                               in1=graw[p][:], op0=ALU.mult, op1=ALU.mult,
                                                               accum_out=ssg[:, c4:c4 + 1]),
                 reads=[f"graw{p}"], writes=[jkey, f"ssg{c4}"])
            rsqrt(rg[:, c4:c4 + 1], ssg[:, c4:c4 + 1], 1.0 / 512, f"ssg{c4}", f"rg{c4}", 1)
            gm = G % 6
            T.op("pool", lambda: nc.gpsimd.tensor_scalar(out=gmix[gm][:], in0=graw[p][:], scalar1=rg[:, c4:c4 + 1],
                                                         scalar2=1.0, op0=ALU.mult, op1=ALU.mult),
                 reads=[f"graw{p}", f"rg{c4}"], writes=[f"gmix{gm}"])

    def E1(c):
        e = c % 2
        for j in (-1, 0, 1):
            for kv in range(2):
                i = kv * 3 + j + 1
                bank = ("b5", "b6", "b7")[(2 * (j + 1) + kv) % 3]
                sl = (c + j) % 6
                T.op("pe", lambda: nc.tensor.matmul(out=PS[bank][:, :], lhsT=KT[64 * kv:64 * kv + 64, sl, :],
                                                    rhs=QT[c % 3][64 * kv:64 * kv + 64, :, :], start=True, stop=True),
                     reads=[f"KT{sl}", f"QT{c % 3}"], writes=[bank])
                T.op("act", lambda: nc.scalar.activation(out=ET[e][:, i, :], in_=PS[bank][:, :], func=AF.Exp,
                                                         scale=0.125),
                     reads=[bank], writes=[f"ET{e}_{i}"])
                T.op("pool", lambda: nc.gpsimd.tensor_tensor(
                    out=ET[e][:, i, :], in0=ET[e][:, i, :],
                    in1=BT[:, j + 1, 4 * kv:4 * kv + 4, :].rearrange("p h q -> p (h q)"), op=ALU.mult),
                     reads=[f"ET{e}_{i}", "BT"], writes=[f"ET{e}_{i}"])

    def E2(c):
        e = c % 2
        c4 = c % 4
        for kv in range(2):
            bank = "b6" if kv == 0 else "b7"
            for g in range(4):
                for j in (-1, 0, 1):
                    i = kv * 3 + j + 1
                    sl = (c + j) % 6
                    T.op("pe", lambda: nc.tensor.matmul(out=PS[bank][:, g * 65:(g + 1) * 65],
                                                        lhsT=ET[e][:, i, g * 128:(g + 1) * 128],
                                                        rhs=VA[:, sl, kv, :], start=(j == -1), stop=(j == 1)),
                         reads=[f"ET{e}_{i}", f"VA{sl}", f"VA{sl}o"], writes=[bank])
        for kv in range(2):
            bank = "b6" if kv == 0 else "b7"
            pv = PS[bank][:, 0:260].rearrange("p (g e) -> p g e", e=65)
            T.op("dve", lambda: nc.vector.tensor_tensor(out=den[e][:, 4 * kv:4 * kv + 4], in0=pv[:, :, 64],
                                                        in1=esink[:, 4 * kv:4 * kv + 4], op=ALU.add),
                 reads=[bank, "esink"], writes=[f"den{e}_{kv}"])
        T.op("dve", lambda: nc.vector.reciprocal(out=rden[e][:], in_=den[e][:]),
             reads=[f"den{e}_0", f"den{e}_1"], writes=[f"rden{e}"])
        for kv in range(2):
            bank = "b6" if kv == 0 else "b7"
            pv = PS[bank][:, 0:260].rearrange("p (g e) -> p g e", e=65)
            T.op("dve", lambda: nc.vector.tensor_tensor(
                out=araw[e][:, 256 * kv:256 * kv + 256].rearrange("p (g d) -> p g d", d=64), in0=pv[:, :, 0:64],
                in1=rden[e][:, 4 * kv:4 * kv + 4].unsqueeze(2).to_broadcast([128, 4, 64]), op=ALU.mult),
                 reads=[bank, f"rden{e}"], writes=[f"araw{e}_{kv}"])
        jk, jkey = junk_next()
        T.op("dve", lambda: nc.vector.scalar_tensor_tensor(out=jk[:, 0:512], in0=araw[e][:], scalar=1.0,
                                                           in1=araw[e][:], op0=ALU.mult, op1=ALU.mult,
                                                           accum_out=ssa[:, c4:c4 + 1]),
             reads=[f"araw{e}_0", f"araw{e}_1"], writes=[jkey, f"ssa{c4}"])
        rsqrt(ra[:, c4:c4 + 1], ssa[:, c4:c4 + 1], 1.0 / 512, f"ssa{c4}", f"ra{c4}", 1)
        T.op("pool", lambda: nc.gpsimd.tensor_scalar(out=amix[e][:], in0=araw[e][:], scalar1=ra[:, c4:c4 + 1],
                                                     scalar2=1.0, op0=ALU.mult, op1=ALU.mult),
             reads=[f"araw{e}_0", f"araw{e}_1", f"ra{c4}"], writes=[f"amix{e}"])

    def F1(c):
        e = c % 2
        gm = c % 6
        for i in range(8):
            src = amix[e][:, i * 128:(i + 1) * 128] if i < 4 else gmix[gm][:, (i - 4) * 128:(i - 3) * 128]
            T.op("pe", lambda: nc.tensor.transpose(out=TR[:, i * 128:(i + 1) * 128], in_=src, identity=identb[:]),
                 reads=[f"amix{e}", f"gmix{gm}", "identb"], writes=["tr"])
        T.op("dve", lambda: nc.vector.tensor_copy(out=mixT[e][:, 0:4, :], in_=TR[:, 0:512]), reads=["tr"],
             writes=[f"mixT{e}a"])
        T.op("dve", lambda: nc.vector.tensor_copy(out=mixT[e][:, 4:8, :], in_=TR[:, 512:1024]), reads=["tr"],
             writes=[f"mixT{e}b"])

    def XR_load(c):
        s, kb, main, m = ginfo(c)
        T.op("sp", lambda: nc.sync.dma_start(out=xr[m % 3][:], in_=xc[s]), writes=[f"xr{m % 3}"],
             slot=f"xr{m % 3}")

    def F2(c):
        s, kb, main, m = ginfo(c)
        e = c % 2
        r = m % 3
        for nh in range(2):
            bank = "b3" if nh == 0 else "b4"
            for kc in range(8):
                T.op("pe", lambda: nc.tensor.matmul(out=PS[bank][:, :], lhsT=mixT[e][:, kc, :],
                                                    rhs=Wo[:, kc, nh * 512:(nh + 1) * 512], start=(kc == 0),
                                                    stop=(kc == 7)),
                     reads=[f"mixT{e}a", f"mixT{e}b", f"Wo{kc}"], writes=[bank])
        for nh in range(2):
            bank = "b3" if nh == 0 else "b4"
            T.op("dve", lambda: nc.vector.tensor_tensor(out=xr[r][:, nh * 512:(nh + 1) * 512], in0=PS[bank][:, :],
                                                        in1=xr[r][:, nh * 512:(nh + 1) * 512], op=ALU.add),
                 reads=[bank, f"xr{r}"], writes=[f"xr{r}"])
        T.op("sp", lambda: nc.sync.dma_start(out=x1_d[m * 128:(m + 1) * 128, :], in_=xr[r][:]),
             reads=[f"xr{r}"], writes=[f"x1d{m}"], slot=f"xs{r}")

    wunits = []
    for (c0, c1) in ((0, 1024), (1024, 2048), (2048, DFF)):
        for wname in ("g", "u"):
            for kc in range(8):
                wunits.append((wname, kc, c0, c1))
    PIECES = ((0, 512), (512, 1024), (1024, 2048), (2048, DFF))
    for fc in range(NFC):
        wunits.append(("d", fc, 0, D))

    def W_load(u):
        wname, kc, c0, c1 = wunits[u]
        i = u % 4
        srcw = {"g": w_gate_d, "u": w_up_d, "d": w_down_d}[wname]
        n = c1 - c0
        T.op("sp", lambda: nc.sync.dma_start(out=wst[i][:, 0:n], in_=srcw[kc * 128:(kc + 1) * 128, c0:c1]),
             writes=[f"wst{i}"], slot=f"wst{i}")

    def W_cast(u):
        wname, kc, c0, c1 = wunits[u]
        i = u % 4
        j = u % 2
        n = c1 - c0
        sc = 1.0 if wname == "d" else g2t[:, kc:kc + 1]
        if u % 2 == 0:
            T.op("pool", lambda: nc.gpsimd.tensor_scalar(out=wcb[j][:, 0:n], in0=wst[i][:, 0:n], scalar1=sc,
                                                         scalar2=1.0, op0=ALU.mult, op1=ALU.mult),
                 reads=[f"wst{i}", "g2t"], writes=[f"wcb{j}"])
        else:
            T.op("act", lambda: nc.scalar.activation(out=wcb[j][:, 0:n], in_=wst[i][:, 0:n], func=AF.Copy, scale=sc),
                 reads=[f"wst{i}", "g2t"], writes=[f"wcb{j}"])

    def W_store(u):
        wname, kc, c0, c1 = wunits[u]
        j = u % 2
        dst = {"g": wg_s, "u": wu_s, "d": wd_s}[wname]
        n = c1 - c0
        T.op("sp", lambda: nc.sync.dma_start(out=dst[:, kc, c0:c1], in_=wcb[j][:, 0:n]),
             reads=[f"wcb{j}"], writes=[f"ws_{wname}_{kc}_{c0}"], slot=f"wcb{j}")

    def W_prefetch(kc):
        for wname, ws, pf in (("g", wg_s, PFg), ("u", wu_s, PFu)):
            T.op("sp", lambda: nc.sync.dma_start(out=pf[:, kc, :], in_=ws[:, kc, 0:512]),
                 reads=[f"ws_{wname}_{kc}_0"], writes=[f"W{wname}0"], slot=f"pf{wname}")

    def is_main(G):
        return 0 <= G < NG and LAYOUT[G] == "main"

    WLAG = 5
    ck('c4')
    for G in range(0, 3):
        if real(G):
            A_load(G)
    for G in range(0, 2):
        if real(G):
            A_comp(G)
    if real(0):
        B(0)
    for t in range(NG + 5 if DEBUG_STEPS is None else DEBUG_STEPS):
        if real(t + 3):
            A_load(t + 3)
        if is_main(t - 2):
            XR_load(t - 2)
        for uu in (2 * (t - WLAG), 2 * (t - WLAG) + 1):
            if 0 <= uu < len(wunits):
                W_load(uu)
        if is_main(t - 4):
            F1(t - 4)
        if real(t + 1):
            B(t + 1)
        if is_main(t - 2):
            E1(t - 2)
        if real(t):
            C(t)
        elif t < NG:
            Virt(t)
        if real(t + 2):
            A_comp(t + 2)
        if is_main(t - 1):
            D2(t - 1)
        if is_main(t - 3):
            E2(t - 3)
        if is_main(t - 4):
            F2(t - 4)
        if real(t):
            Dst(t)
        for uu in (2 * (t - WLAG - 2), 2 * (t - WLAG - 2) + 1):
            if 0 <= uu < len(wunits):
                W_store(uu)
        for uu in (2 * (t - WLAG - 1), 2 * (t - WLAG - 1) + 1):
            if 0 <= uu < len(wunits):
                W_cast(uu)
        if 0 <= t - (14 + WLAG) < 8:
            W_prefetch(t - (14 + WLAG))

    T.barrier()
    esA.close()
    stacks.pop()

    if not DEBUG_X1:
        Wgp = [PFg] + [sb(f"Wg{pi}", [128, 8, c1 - c0], BF16) for pi, (c0, c1) in enumerate(PIECES) if pi > 0]
        Wup = [PFu] + [sb(f"Wu{pi}", [128, 8, c1 - c0], BF16) for pi, (c0, c1) in enumerate(PIECES) if pi > 0]
        Wd = sb("Wd", [128, NFC, D], BF16)
        x1p = [sb(f"x1p{i}", [128, D], F32) for i in range(7)]
        h2 = [sb(f"h2_{i}", [128, D], BF16) for i in range(2)]
        h2T = sb("h2T", [128, 8, GRP * 128], BF16)
        actT = sb("actT", [128, NFC, GRP * 128], BF16)
        sg = [sb(f"sg{i}", [128, GRP * 128], F32) for i in range(2)]
        ss2 = sb("ss2", [128, 4], F32)
        rs2 = sb("rs2", [128, 4], F32)

        NGRP = NM // GRP
        banksB = ["b1", "b2", "b3", "b4", "b5", "b6", "b7"]
        bctr = [0]

        def nbank():
            b = banksB[bctr[0] % 7]
            bctr[0] += 1
            return b

        def P_load(m):
            r = m % 7
            T.op("sp", lambda: nc.sync.dma_start(out=x1p[r][:], in_=x1_d[m * 128:(m + 1) * 128, :]),
                 reads=[f"x1d{m}"], writes=[f"x1p{r}"], slot=f"x1p{r}")

        def P_norm(m):
            r = m % 7
            c = m % 4
            jk, jkey = junk_next()
            T.op("dve", lambda: nc.vector.scalar_tensor_tensor(out=jk[:], in0=x1p[r][:], scalar=1.0,
                                                               in1=x1p[r][:], op0=ALU.mult, op1=ALU.mult,
                                                               accum_out=ss2[:, c:c + 1]),
                 reads=[f"x1p{r}"], writes=[jkey, f"ss2_{c}"])
            rsqrt(rs2[:, c:c + 1], ss2[:, c:c + 1], 1.0 / D, f"ss2_{c}", f"rs2_{c}", 1)
            T.op("dve", lambda: nc.vector.tensor_scalar(out=h2[m % 2][:], in0=x1p[r][:], scalar1=rs2[:, c:c + 1],
                                                        scalar2=None, op0=ALU.mult),
                 reads=[f"x1p{r}", f"rs2_{c}"], writes=[f"h2_{m % 2}"])

        def P_tr(m):
            i4 = m % GRP
            for i in range(8):
                T.op("pe", lambda: nc.tensor.transpose(out=TR[:, i * 128:(i + 1) * 128],
                                                       in_=h2[m % 2][:, i * 128:(i + 1) * 128], identity=identb[:]),
                     reads=[f"h2_{m % 2}", "identb"], writes=["tr"])
            T.op("dve", lambda: nc.vector.tensor_copy(
                out=h2T[:, 0:4, i4 * 128:(i4 + 1) * 128],
                in_=TR[:, 0:512].rearrange("p (c t) -> p c t", t=128)), reads=["tr"], writes=["h2Ta"])
            T.op("dve", lambda: nc.vector.tensor_copy(
                out=h2T[:, 4:8, i4 * 128:(i4 + 1) * 128],
                in_=TR[:, 512:1024].rearrange("p (c t) -> p c t", t=128)), reads=["tr"], writes=["h2Tb"])

        def GU(g):
            for fc in range(NFC):
                pi = [p_ for p_, (c0, c1) in enumerate(PIECES) if c0 <= fc * 128 < c1][0]
                pc0 = PIECES[pi][0]
                bg = nbank()
                bu = nbank()
                for bank, wt, wname in ((bg, Wgp[pi], "Wg"), (bu, Wup[pi], "Wu")):
                    for kc in range(8):
                        T.op("pe", lambda: nc.tensor.matmul(out=PS[bank][:, :],
                                                            lhsT=wt[:, kc, fc * 128 - pc0:(fc + 1) * 128 - pc0],
                                                            rhs=h2T[:, kc, :], start=(kc == 0), stop=(kc == 7)),
                             reads=[f"{wname}{pi}", "h2Ta", "h2Tb"], writes=[bank])
                T.op("act", lambda: nc.scalar.activation(out=sg[fc % 2][:], in_=PS[bg][:, :], func=AF.Silu),
                     reads=[bg], writes=[f"sg{fc % 2}"])
                T.op("dve", lambda: nc.vector.tensor_tensor(out=actT[:, fc, :], in0=PS[bu][:, :], in1=sg[fc % 2][:],
                                                            op=ALU.mult),
                     reads=[bu, f"sg{fc % 2}"], writes=[f"actT{fc}"])

        def DN_tile(g, i4):
            m = g * GRP + i4
            r = m % 7
            bks = [nbank(), nbank()]
            for nh in range(2):
                for fc in range(NFC):
                    T.op("pe", lambda: nc.tensor.matmul(out=PS[bks[nh]][:, :],
                                                        lhsT=actT[:, fc, i4 * 128:(i4 + 1) * 128],
                                                        rhs=Wd[:, fc, nh * 512:(nh + 1) * 512], start=(fc == 0),
                                                        stop=(fc == NFC - 1)),
                         reads=[f"actT{fc}", f"Wd{fc // 11}"], writes=[bks[nh]])
            for nh in range(2):
                T.op("dve", lambda: nc.vector.tensor_tensor(out=x1p[r][:, nh * 512:(nh + 1) * 512],
                                                            in0=PS[bks[nh]][:, :],
                                                            in1=x1p[r][:, nh * 512:(nh + 1) * 512], op=ALU.add),
                     reads=[bks[nh], f"x1p{r}"], writes=[f"x1p{r}"])
            T.op("sp", lambda: nc.sync.dma_start(out=y_d[m * 128:(m + 1) * 128, :], in_=x1p[r][:]),
                 reads=[f"x1p{r}"], writes=[f"y{m}"], slot=f"ys{r}")

        for m in range(GRP):
            P_load(m)
        for pi, (c0, c1) in enumerate(PIECES):
            if pi == 0:
                continue
            for wname, ws, wt in (("g", wg_s, Wgp[pi]), ("u", wu_s, Wup[pi])):
                T.op("sp", lambda: nc.sync.dma_start(out=wt[:], in_=ws[:, :, c0:c1]),
                     writes=[f"W{wname}{pi}"], slot=f"W{wname}{pi}")
        for hh in range(2):
            T.op("sp", lambda: nc.sync.dma_start(out=Wd[:, hh * 11:(hh + 1) * 11, :], in_=wd_s[:, hh * 11:(hh + 1) * 11, :]),
                 writes=[f"Wd{hh}"], slot=f"Wd{hh}")

        for m in range(GRP):
            P_norm(m)
            P_tr(m)
        for g in range(NGRP):
            nxt = g + 1 < NGRP
            m0 = (g + 1) * GRP
            if nxt:
                for i4 in range(3):
                    P_load(m0 + i4)
            GU(g)
            if nxt:
                for i4 in range(3):
                    P_norm(m0 + i4)
                    P_tr(m0 + i4)
            for i4 in range(GRP):
                DN_tile(g, i4)
                if nxt and i4 == 0:
                    P_load(m0 + 3)
                    P_norm(m0 + 3)
                    P_tr(m0 + 3)

    return


def t5_buckets_np(rel):
    half = 16
    max_exact = 8
    ret = (rel > 0).astype(np.int32) * half
    n = np.abs(rel)
    large = max_exact + (np.log(np.maximum(n, 1).astype(np.float32) / max_exact)
                         / np.log(128 / max_exact) * (half - max_exact)).astype(np.int32)
    large = np.minimum(large, half - 1)
    return (ret + np.where(n < max_exact, n, large)).astype(np.int32)


_CACHE = {}


def kernel(x_prompt, x_sample, rel_bias_table, norm1, w_in, q_gain, k_gain, sink, v_gain, w_s, b_s,
           attn_out_gain, gmlp_out_gain, w_o, norm2, w_gate, w_up, w_down):
    f = np.float32
    x_prompt = np.asarray(x_prompt, f)
    x_sample = np.asarray(x_sample, f)
    seqs = [x_prompt[i] for i in range(x_prompt.shape[0])] + [x_sample[i] for i in range(x_sample.shape[0])]
    assert len(seqs) == 12

    w_in0 = np.asarray(w_in, f)[0]
    hperm = [0, 4, 1, 5, 2, 6, 3, 7]
    qcols = np.concatenate([np.arange(h * 64, (h + 1) * 64) for h in hperm])
    w_in_p = np.ascontiguousarray(np.concatenate([w_in0[:, qcols], w_in0[:, 512:]], axis=1))
    col = lambda v: np.ascontiguousarray(np.asarray(v, f).reshape(8, 128).T)
    g1 = col(np.asarray(norm1, f)[0])
    g2 = col(np.asarray(norm2, f)[0])
    gm = col(np.concatenate([np.asarray(attn_out_gain, f)[0], np.asarray(gmlp_out_gain, f)[0]]))
    qg = np.ascontiguousarray(np.tile(np.asarray(q_gain, f)[0], 2).reshape(128, 1))
    kg = np.ascontiguousarray(np.tile(np.asarray(k_gain, f)[0], 2).reshape(128, 1))
    vgb = np.ascontiguousarray(np.broadcast_to(np.asarray(v_gain, f)[0][None, :], (128, 512)))
    bsb = np.ascontiguousarray(np.repeat(np.asarray(b_s, f)[0].T[:, :, None], 64, axis=2).reshape(128, 512))
    wsT = np.ascontiguousarray(np.transpose(np.asarray(w_s, f)[0], (2, 0, 1)))
    tab = np.ascontiguousarray(np.asarray(rel_bias_table, f))
    sinkb = np.ascontiguousarray(np.broadcast_to(np.asarray(sink, f)[0][None, :], (128, 8)))
    ident = np.eye(128, dtype=f)
    oht = np.zeros((33, 512), f)
    for i in range(512):
        r = 255 - i
        if abs(r) <= 128:
            oht[int(t5_buckets_np(np.array([r]))[0]), i] = 1.0
        else:
            oht[32, i] = 1.0

    shared = dict(w_in=w_in_p, w_o=np.ascontiguousarray(np.asarray(w_o, f)[0]),
                  w_gate=np.ascontiguousarray(np.asarray(w_gate, f)[0]),
                  w_up=np.ascontiguousarray(np.asarray(w_up, f)[0]),
                  w_down=np.ascontiguousarray(np.asarray(w_down, f)[0]),
                  g1=g1, g2=g2, gm=gm, qg=qg, kg=kg, vgb=vgb, bsb=bsb, wsT=wsT, tab=tab, oht=oht,
                  sinkb=sinkb, ident=ident, jrev=np.ascontiguousarray(ident[::-1]))

    in_maps = []
    for c in range(8):
        xcore = np.zeros((NX, 128, D), f)
        flags = np.zeros((128, NF), f)
        xcore[0:32] = seqs[c].reshape(32, 128, D)
        hf = c % 2
        xs = seqs[8 + c // 2].reshape(32, 128, D)
        xcore[33:49] = xs[16 * hf:16 * hf + 16]
        if hf == 1:
            xcore[32] = xs[15]
            flags[:, 0] = 1.0
        else:
            xcore[49] = xs[16]
            flags[:, 1] = 1.0
        m = dict(shared)
        m["xc"] = xcore
        m["flags"] = flags
        in_maps.append(m)

    if "nc" not in _CACHE:
        _CACHE["nc"] = build_program()[0]
    nc = _CACHE["nc"]
    res = run_bass_kernel_spmd(nc, in_maps, core_ids=list(range(8)))
    key = "x1d" if DEBUG_X1 else "y"
    outs = [np.asarray(r[key]).reshape(NM * 128, D) for r in res.results]
    full = [np.zeros((4096, D), f) for _ in seqs]
    for c in range(8):
        hf = c % 2
        full[c][:] = outs[c][0:4096]
        full[8 + c // 2][2048 * hf:2048 * hf + 2048] = outs[c][4096:6144]
    yp = np.stack(full[:x_prompt.shape[0]]).astype(f)
    ys = np.stack(full[x_prompt.shape[0]:]).astype(f)
    return (yp, ys)
```
